# Optimizing a Trainium2 kernel written in Bass

```python
import math
import jax, jax.numpy as jnp
from jax import lax
import numpy as np

D_MODEL = 1024
BATCH = 1
SEQ = 16384
DEPTH = 1
DEC_BATCH = 4
DEC_SEQ = 8192
PAST_LEN = 128

D_MIX = D_MODEL
D_ATTN = D_MIX // 2
D_SGU = D_MIX - D_ATTN
N_HEADS = 4
HEAD_DIM = D_ATTN // (2 * N_HEADS)
V_DIM = 2 * HEAD_DIM
N_SGU_GROUPS = 4
SGU_GROUP = D_SGU // N_SGU_GROUPS
CHUNK = 128
Q_BLOCK = 128
ROPE_THETA = 10000.0
EPS = 1e-6
D_IN_PROJ = 3 * D_ATTN + D_ATTN + 2 * D_SGU + D_SGU
SPLITS = tuple(np.cumsum([D_ATTN, D_ATTN, D_ATTN, D_ATTN, D_SGU, D_SGU])[:].tolist())

kernel_name = "hybrid_diffattn_sgu_encoder"


def _lambda_init(layer_idx):
    return 0.8 - 0.6 * math.exp(-0.3 * layer_idx)


def rms_norm(x, w):
    xf = x.astype(jnp.float32)
    y = xf * lax.rsqrt(jnp.mean(xf * xf, axis=-1, keepdims=True) + EPS)
    return (y * w.astype(jnp.float32)).astype(x.dtype)


def layer_norm(x, w, b):
    xf = x.astype(jnp.float32)
    mu = jnp.mean(xf, axis=-1, keepdims=True)
    var = jnp.mean(jnp.square(xf - mu), axis=-1, keepdims=True)
    y = (xf - mu) * lax.rsqrt(var + EPS)
    return (y * w.astype(jnp.float32) + b.astype(jnp.float32)).astype(x.dtype)


def rope(x, seq_len):
    half = HEAD_DIM // 2
    inv_freq = 1.0 / (ROPE_THETA ** (jnp.arange(half, dtype=jnp.float32) / half))
    ang = jnp.arange(seq_len, dtype=jnp.float32)[:, None] * inv_freq[None, :]
    ang = jnp.concatenate([ang, ang], axis=-1)
    cos = jnp.cos(ang)[None, :, None, :].astype(x.dtype)
    sin = jnp.sin(ang)[None, :, None, :].astype(x.dtype)
    x1, x2 = x[..., :half], x[..., half:]
    rot = jnp.concatenate([-x2, x1], axis=-1)
    return x * cos + rot * sin


def diff_attention(q, k, v, q_norm_w, k_norm_w, lambda_qk, subln_w, lambda_init):
    B, S, _ = q.shape
    q = q.reshape(B, S, 2 * N_HEADS, HEAD_DIM)
    k = k.reshape(B, S, 2 * N_HEADS, HEAD_DIM)
    v = v.reshape(B, S, N_HEADS, V_DIM)
    q = rope(rms_norm(q, q_norm_w), S) * (HEAD_DIM ** -0.5)
    k = rope(rms_norm(k, k_norm_w), S)
    lq = lambda_qk.astype(jnp.float32)
    lam = jnp.exp(jnp.sum(lq[0] * lq[1])) - jnp.exp(jnp.sum(lq[2] * lq[3])) + lambda_init
    nb = S // Q_BLOCK
    qb = q.reshape(B, nb, Q_BLOCK, 2 * N_HEADS, HEAD_DIM).transpose(1, 0, 3, 2, 4)
    kt = k.transpose(0, 2, 1, 3)
    vt = v.transpose(0, 2, 1, 3)

    def block(q_blk):
        s = jnp.einsum('bhqd,bhkd->bhqk', q_blk, kt).astype(jnp.float32)
        p = jax.nn.softmax(s, axis=-1).reshape(B, N_HEADS, 2, Q_BLOCK, S)
        p_diff = (p[:, :, 0] - lam * p[:, :, 1]).astype(vt.dtype)
        return jnp.einsum('bhqk,bhkv->bhqv', p_diff, vt)

    o = lax.map(block, qb)
    o = o.transpose(1, 0, 3, 2, 4).reshape(B, S, N_HEADS, V_DIM)
    o = rms_norm(o, subln_w) * (1.0 - lambda_init)
    return o.reshape(B, S, D_ATTN)


def spatial_gating(u, vg, sgu_norm_w, sgu_norm_b, w_spatial, b_spatial):
    B, S, _ = u.shape
    vn = layer_norm(vg, sgu_norm_w, sgu_norm_b)
    vc = vn.reshape(B, S // CHUNK, CHUNK, N_SGU_GROUPS, SGU_GROUP)
    mixed = jnp.einsum('gij,bnjgc->bnigc', w_spatial, vc) + b_spatial.T[None, None, :, :, None]
    return u * mixed.reshape(B, S, D_SGU)


def hybrid_layer(x, c, norm_w, w_ada, b_ada, w_in, w_out, q_norm_w, k_norm_w, lambda_qk,
                 subln_w, sgu_norm_w, sgu_norm_b, w_spatial, b_spatial, lambda_init):
    mod = jnp.einsum('bd,de->be', jax.nn.silu(c), w_ada) + b_ada
    shift, scale, gate = jnp.split(mod, 3, axis=-1)
    h = rms_norm(x, norm_w) * (1.0 + scale[:, None, :]) + shift[:, None, :]
    proj = jnp.einsum('bsd,de->bse', h, w_in)
    q, k, v, z_a, u, vg, z_s = jnp.split(proj, SPLITS, axis=-1)
    a = diff_attention(q, k, v, q_norm_w, k_norm_w, lambda_qk, subln_w, lambda_init) * jax.nn.silu(z_a)
    s = spatial_gating(u, vg, sgu_norm_w, sgu_norm_b, w_spatial, b_spatial) * jax.nn.silu(z_s)
    y = jnp.einsum('bse,ed->bsd', jnp.concatenate([a, s], axis=-1), w_out)
    return x + gate[:, None, :] * y


def setup_inputs(seed: int = 0) -> dict:
    key = jax.random.key(seed)
    ks = jax.random.split(key, 18)
    f32 = jnp.float32
    nrm = lambda k, shape, s: jax.random.normal(k, shape, f32) * s
    return {
        "x_prompt": nrm(ks[0], (BATCH, SEQ, D_MODEL), 1.0),
        "x_sample": nrm(ks[1], (DEC_BATCH, DEC_SEQ, D_MODEL), 1.0),
        "c_prompt": nrm(ks[2], (BATCH, D_MODEL), 1.0),
        "c_sample": nrm(ks[3], (DEC_BATCH, D_MODEL), 1.0),
        "norm_w": 1.0 + nrm(ks[4], (DEPTH, D_MODEL), 0.02),
        "w_ada": nrm(ks[5], (DEPTH, D_MODEL, 3 * D_MODEL), D_MODEL ** -0.5),
        "b_ada": nrm(ks[6], (DEPTH, 3 * D_MODEL), 0.01),
        "w_in": nrm(ks[7], (DEPTH, D_MODEL, D_IN_PROJ), D_MODEL ** -0.5),
        "w_out": nrm(ks[8], (DEPTH, D_MIX, D_MODEL), D_MIX ** -0.5),
        "q_norm_w": 1.0 + nrm(ks[9], (DEPTH, HEAD_DIM), 0.02),
        "k_norm_w": 1.0 + nrm(ks[10], (DEPTH, HEAD_DIM), 0.02),
        "lambda_qk": nrm(ks[11], (DEPTH, 4, HEAD_DIM), 0.1),
        "subln_w": 1.0 + nrm(ks[12], (DEPTH, V_DIM), 0.02),
        "sgu_norm_w": 1.0 + nrm(ks[13], (DEPTH, D_SGU), 0.02),
        "sgu_norm_b": nrm(ks[14], (DEPTH, D_SGU), 0.01),
        "w_spatial": nrm(ks[15], (DEPTH, N_SGU_GROUPS, CHUNK, CHUNK), CHUNK ** -0.5),
        "b_spatial": 1.0 + nrm(ks[16], (DEPTH, N_SGU_GROUPS, CHUNK), 0.02),
    }


def reference(x_prompt, x_sample, c_prompt, c_sample, norm_w, w_ada, b_ada, w_in, w_out,
              q_norm_w, k_norm_w, lambda_qk, subln_w, sgu_norm_w, sgu_norm_b, w_spatial, b_spatial):
    y_prompt = x_prompt
    y_sample = x_sample
    for l in range(DEPTH):
        lam0 = _lambda_init(l)
        params = (norm_w[l], w_ada[l], b_ada[l], w_in[l], w_out[l], q_norm_w[l], k_norm_w[l],
                  lambda_qk[l], subln_w[l], sgu_norm_w[l], sgu_norm_b[l], w_spatial[l], b_spatial[l])
        y_prompt = hybrid_layer(y_prompt, c_prompt, *params, lam0)
        y_sample = hybrid_layer(y_sample, c_sample, *params, lam0)
    return (y_prompt, y_sample)
```

```python
import numpy as np
import concourse.bass as bass
import concourse.mybir as mybir

F32 = mybir.dt.float32
BF16 = mybir.dt.bfloat16
ALU = mybir.AluOpType
AF = mybir.ActivationFunctionType
AX = mybir.AxisListType

COMPUTE = ("pe", "act", "dve", "pool")


class Res:
    __slots__ = ("name", "writers", "readers", "prev_readers")

    def __init__(self, name):
        self.name = name
        self.writers = []
        self.readers = []
        self.prev_readers = []


class V:
    __slots__ = ("ap", "res")

    def __init__(self, ap, res):
        self.ap = ap
        self.res = res if isinstance(res, (list, tuple)) else [res]

    def __getitem__(self, k):
        return V(self.ap[k], self.res)

    def re(self, s, **kw):
        return V(self.ap.rearrange(s, **kw), self.res)


class Op:
    __slots__ = ("eng", "fn", "deps", "dma", "signal", "count", "sem", "idx", "nm", "partial")


class Prog:
    def __init__(self, nc, n_dma_sems=20):
        self.nc = nc
        self.ops = []
        self.n_dma_sems = n_dma_sems
        self.last = {}
        self.pending = {}
        self.dmas_since_barrier = []

    def add(self, eng, fn, reads=(), writes=(), dma=False, partial=False, nm=""):
        op = Op()
        op.eng, op.fn, op.dma, op.nm = eng, fn, dma, nm
        op.idx = len(self.ops)
        op.signal = False
        op.count = None
        op.sem = None
        op.partial = partial
        deps = set()
        rres, wres = [], []
        for v in reads:
            for r in (v.res if isinstance(v, V) else [v]):
                if r not in rres:
                    rres.append(r)
        for v in writes:
            for r in (v.res if isinstance(v, V) else [v]):
                if r not in wres:
                    wres.append(r)
        for r in rres:
            deps.update(r.writers)
        for r in wres:
            deps.update(r.readers)
            deps.update(r.prev_readers)
            for w in r.writers:
                if not (partial and w.partial):
                    deps.add(w)
            if r.readers:
                r.prev_readers = r.readers
                r.readers = []
                r.writers = [op]
            else:
                if partial:
                    r.writers.append(op)
                else:
                    r.writers = [op]
                    r.prev_readers = []
        for r in rres:
            r.readers.append(op)
        if self.pending.get(eng):
            deps.update(self.pending.pop(eng))
        deps.discard(op)
        if dma:
            self.dmas_since_barrier.append(op)
        self.last[eng] = op
        keep = []
        for d in deps:
            if d.eng == eng and eng == "pe" and not d.dma and not dma:
                continue
            keep.append(d)
        op.deps = keep
        self.ops.append(op)
        return op

    def barrier(self):
        allv = list(self.last.values()) + list(self.dmas_since_barrier)
        for e in COMPUTE + ("sp",):
            self.pending[e] = set(allv) | set(self.pending.get(e, ()))
        self.dmas_since_barrier = []

    def emit(self, final_wait_eng="sp"):
        nc = self.nc
        ops = self.ops
        for op in ops:
            for d in op.deps:
                d.signal = True
        for op in ops:
            if op.dma:
                op.signal = True
        cnt = {e: 0 for e in COMPUTE + ("sp",)}
        dma_rr = {}
        dma_sem_cnt = {}
        dma_last_on_sem = {}
        import contextlib
        es = contextlib.ExitStack()
        with es:
            sems = {e: es.enter_context(nc.semaphore("s_" + e)) for e in COMPUTE}
            dsems = {}
            for e in ("sp", "pool", "act"):
                dsems[e] = [es.enter_context(nc.semaphore(f"d_{e}{i}")) for i in range(self.n_dma_sems if e == "sp" else 12)]
                dma_rr[e] = 0
            extra_dma_wait = {}
            for op in ops:
                if op.dma:
                    lst = dsems[op.eng]
                    k = dma_rr[op.eng] % len(lst)
                    dma_rr[op.eng] += 1
                    s = lst[k]
                    dma_sem_cnt[s] = dma_sem_cnt.get(s, 0) + 16
                    op.sem, op.count = s, dma_sem_cnt[s]
                    prev = dma_last_on_sem.get(s)
                    if prev is not None:
                        extra_dma_wait[op.idx] = prev
                    dma_last_on_sem[s] = op
                elif op.signal:
                    cnt[op.eng] += 1
                    op.sem, op.count = sems[op.eng], cnt[op.eng]
            self.max_counts = dict(cnt)
            byeng = {}
            for op in ops:
                byeng.setdefault(op.eng, []).append(op)

            def run(engname, eng):
                waited = {}
                for op in byeng.get(engname, []):
                    need = {}
                    dl = list(op.deps)
                    if op.idx in extra_dma_wait:
                        dl.append(extra_dma_wait[op.idx])
                    for d in dl:
                        if d.sem is None:
                            continue
                        if need.get(d.sem, 0) < d.count:
                            need[d.sem] = d.count
                    for s, c in need.items():
                        if waited.get(s, 0) < c:
                            eng.wait_ge(s, c)
                            waited[s] = c
                    ins = op.fn(eng)
                    if op.sem is not None:
                        ins.then_inc(op.sem, 16 if op.dma else 1)
                if engname == final_wait_eng:
                    for s, c in dma_sem_cnt.items():
                        if waited.get(s, 0) < c:
                            eng.wait_ge(s, c)
                    for e2 in COMPUTE:
                        if cnt[e2] > 0:
                            eng.wait_ge(sems[e2], cnt[e2])

            with nc.Block() as block:
                @block.tensor
                def _(e):
                    run("pe", e)

                @block.scalar
                def _(e):
                    run("act", e)

                @block.vector
                def _(e):
                    run("dve", e)

                @block.gpsimd
                def _(e):
                    run("pool", e)

                @block.sync
                def _(e):
                    run("sp", e)

    def mm(self, out, lhsT, rhs, start=True, stop=True, tp=None, nm="mm"):
        kw = {}
        if tp is not None:
            kw["tile_position"] = tp
        return self.add("pe", lambda e: e.matmul(out.ap, lhsT.ap, rhs.ap, start=start, stop=stop, **kw),
                        reads=[lhsT, rhs], writes=[out], partial=True, nm=nm)

    def tr(self, out, in_, ident, nm="tr"):
        return self.add("pe", lambda e: e.transpose(out.ap, in_.ap, ident.ap),
                        reads=[in_, ident], writes=[out], partial=True, nm=nm)

    def actf(self, out, in_, func, bias=None, scale=1.0, accum=None, eng="act", nm="act"):
        reads = [in_]
        kw = {}
        if bias is not None:
            if isinstance(bias, V):
                reads.append(bias)
                kw["bias"] = bias.ap
            else:
                kw["bias"] = bias
        if isinstance(scale, V):
            reads.append(scale)
            kw["scale"] = scale.ap
        else:
            kw["scale"] = scale
        writes = [out]
        if accum is not None:
            writes.append(accum)
            kw["accum_out"] = accum.ap
        return self.add("act", lambda e: e.activation(out.ap, in_.ap, func, **kw), reads=reads, writes=writes, nm=nm)

    def tt(self, out, in0, in1, op, eng="dve", nm="tt"):
        return self.add(eng, lambda e: e.tensor_tensor(out.ap, in0.ap, in1.ap, op), reads=[in0, in1], writes=[out], nm=nm)

    def ts(self, out, in0, s1, s2=None, op0=ALU.mult, op1=None, eng="dve", accum=None, nm="ts"):
        reads = [in0]
        a1 = s1.ap if isinstance(s1, V) else s1
        a2 = s2.ap if isinstance(s2, V) else s2
        if isinstance(s1, V):
            reads.append(s1)
        if isinstance(s2, V):
            reads.append(s2)
        kw = {}
        if op1 is not None:
            kw["op1"] = op1
        writes = [out]
        if accum is not None:
            writes.append(accum)
            kw["accum_out"] = accum.ap
        return self.add(eng, lambda e: e.tensor_scalar(out.ap, in0.ap, a1, a2, op0, **kw), reads=reads, writes=writes, nm=nm)

    def stt(self, out, in0, scalar, in1, op0, op1, eng="dve", nm="stt"):
        reads = [in0, in1]
        a = scalar.ap if isinstance(scalar, V) else scalar
        if isinstance(scalar, V):
            reads.append(scalar)
        return self.add(eng, lambda e: e.scalar_tensor_tensor(out.ap, in0.ap, a, in1.ap, op0, op1), reads=reads, writes=[out], nm=nm)

    def copy(self, out, in_, eng="dve", nm="copy"):
        if eng == "act":
            return self.add("act", lambda e: e.copy(out.ap, in_.ap), reads=[in_], writes=[out], nm=nm)
        return self.add(eng, lambda e: e.tensor_copy(out.ap, in_.ap), reads=[in_], writes=[out], nm=nm)

    def red(self, out, in_, op=ALU.add, axis=AX.X, eng="dve", nm="red", **kw):
        return self.add(eng, lambda e: e.tensor_reduce(out.ap, in_.ap, axis, op, **kw), reads=[in_], writes=[out], nm=nm)

    def recip(self, out, in_, nm="recip"):
        return self.add("dve", lambda e: e.reciprocal(out.ap, in_.ap), reads=[in_], writes=[out], nm=nm)

    def memset(self, out, val, eng="dve", nm="memset"):
        return self.add(eng, lambda e: e.memset(out.ap, val), writes=[out], nm=nm)

    def dma(self, out, in_, eng="sp", partial=True, nm="dma", **kw):
        return self.add(eng, lambda e: e.dma_start(out=out.ap, in_=in_.ap, **kw), reads=[in_], writes=[out], dma=True,
                        partial=partial, nm=nm)


import contextlib
from concourse.bass_utils import run_bass_kernel_spmd

D = 1024
EPS = 1e-6
LAM0 = 0.8 - 0.6 * 1.0
N_CORES = 8


def build(SP, SS, PIECE=1024):
    NQ = [SP // 8, SS // 2]
    NK = [SP, SS]
    SMAX = max(SP, SS)
    nc = bass.Bass("TRN2", target_bir_lowering=False)
    din = lambda n, shp: nc.dram_tensor(n, shp, F32, kind="ExternalInput").ap()
    xkv_d = [din("xkv_p", [SP, D]), din("xkv_s", [SS, D])]
    xq_d = [din("xq_p", [NQ[0], D]), din("xq_s", [NQ[1], D])]
    ropek_d = din("rope_k", [SMAX, 2, 64])
    ropeq_d = [din("rope_qp", [NQ[0], 2, 64]), din("rope_qs", [NQ[1], 2, 64])]
    cT_d = din("cT", [128, 8, 2])
    wada_d = din("w_ada", [D, 3 * D])
    badaT_d = din("b_adaT", [128, 24])
    normwT_d = din("norm_wT", [128, 8])
    win_d = din("w_in", [D, 3584])
    wout_d = din("w_out", [D, D])
    qkw_d = din("qkw", [128])
    lam_d = din("lamqk", [256])
    subw_d = din("subw", [128])
    sgunw_d = din("sgunw", [1024])
    wspT_d = din("wspT", [128, 4, 128])
    bspT_d = din("bspT", [128, 4])
    ident_d = din("ident", [128, 128])
    y_d = [nc.dram_tensor("y_p", [NQ[0], D], F32, kind="ExternalOutput").ap(),
           nc.dram_tensor("y_s", [NQ[1], D], F32, kind="ExternalOutput").ap()]
    kt_scr = [nc.dram_tensor(f"kt_scr{g}", [4, 128, NK[g]], BF16).ap() for g in range(2)]
    v_scr = [nc.dram_tensor(f"v_scr{g}", [4, 128, NK[g] // 128, 128], BF16).ap() for g in range(2)]

    P = Prog(nc)
    es = contextlib.ExitStack()
    with es:
        def sb(name, shape, dt=F32):
            h = es.enter_context(nc.sbuf_tensor("sb_" + name, shape, dt))
            return V(h[:], Res(name))
        DR = lambda ap, n: V(ap, Res(n))

        pp = [es.enter_context(nc.psum_tensor(f"pp{i}", [128, 1024], F32)) for i in range(4)]
        bres = [Res(f"bank{i}") for i in range(8)]
        bank = [V(pp[b // 2][:, (b % 2) * 512:(b % 2) * 512 + 512], bres[b]) for b in range(8)]
        bankb = [V(pp[b // 2][:, (b % 2) * 512:(b % 2) * 512 + 512].bitcast(BF16), bres[b]) for b in range(8)]
        Sps = [V(pp[0][:, :], [bres[0], bres[1]]), V(pp[1][:, :], [bres[2], bres[3]])]
        rrc = {"p": 0, "a": 0, "h": 0}
        sets = {"p": [0, 1, 2, 3], "a": [4], "h": [5, 6, 7]}

        def nb(kind="a"):
            rrc[kind] += 1
            st_ = sets[kind]
            return st_[rrc[kind] % len(st_)]

        ident = sb("ident", [128, 128]); identb = sb("identb", [128, 128], BF16)
        ones_bf = sb("ones_bf", [128, 32], BF16); onesf = sb("onesf", [128, 128])
        mhalf = sb("mhalf", [128, 16])
        qkw = sb("qkw", [128, 2, 64]); wtab = sb("wtab", [128, 2, 2, 64])
        lamt = sb("lamt", [128, 2, 2, 64]); lsm = sb("lsm", [128, 8])
        negB = sb("negB", [128, 1]); neglam = sb("neglam", [128, 1])
        subw = sb("subw", [128, 128]); sgu = sb("sgu", [128, 2, 512])
        wspf = sb("wspf", [128, 4, 128]); wsp = sb("wsp", [128, 4, 128], BF16); bsp = sb("bsp", [128, 4])
        cT = sb("cT", [128, 8, 2]); siluc = sb("siluc", [128, 8, 2]); badaT = sb("badaT", [128, 24]); normwT = sb("normwT", [128, 8])
        modT = sb("modT", [128, 24, 2]); gT = sb("gT", [128, 8, 2])
        gate_bc = sb("gate_bc", [128, 2, D])
        Wout = sb("Wout", [128, 8, D], BF16)
        xs = [sb(f"xs{i}", [128, 4, D]) for i in range(2)]
        arena_h = es.enter_context(nc.sbuf_tensor("sb_arena", [128, 8 * 2560], BF16))
        ar_res = Res("arena")
        Wqr = V(arena_h[:, :].rearrange("p (k c) -> p k c", k=8), ar_res)
        Wkv = V(arena_h[:, 0:8192].rearrange("p (k c) -> p k c", k=8), Res("Wkv"))
        KTs = [V(arena_h[:, 8192 + i * 2048: 8192 + (i + 1) * 2048].rearrange("p (h n) -> p h n", h=4), Res(f"KTs{i}")) for i in range(2)]
        Vs = [V(arena_h[:, 12288 + i * 2048: 12288 + (i + 1) * 2048].rearrange("p (h t v) -> p h t v", h=4, t=4), Res(f"Vs{i}")) for i in range(2)]
        xn = sb("xn", [128, 4, D], BF16)
        hTs = [sb(f"hT{i}", [128, 8, 512], BF16) for i in range(2)]
        sqs = [sb(f"sq{i}", [128, 512]) for i in range(2)]; sq = sqs[0]; t1 = sb("t1", [128, 512]); t2 = sb("t2", [128, 512])
        krs = [sb(f"kr{i}", [128, 512], BF16) for i in range(3)]; kr = krs[0]
        ropes = [sb(f"ropes{i}", [128, 4, 2, 64]) for i in range(2)]
        ABs = [sb(f"AB{i}", [128, 4, 2, 64]) for i in range(2)]
        st4 = sb("st4", [128, 4]); rs4 = sb("rs4", [128, 4]); st4b = sb("st4b", [128, 4]); rs4b = sb("rs4b", [128, 4]); st8s = [sb(f"st8{i}", [128, 8]) for i in range(2)]; rk8s = [sb(f"rk8{i}", [128, 8]) for i in range(2)]
        lns = sb("lns", [128, 8])
        QT = sb("QT", [128, 4, 512], BF16)
        za = sb("za", [128, 4, 512], BF16)
        catT = sb("catT", [128, 8, 512], BF16)
        s_bf = sb("s_bf", [128, 512], BF16)
        o_store = sb("o_store", [128, 4, 512])
        u_sb = sb("u_sb", [128, 512]); th = sb("th", [128, 512]); vnb = sb("vnb", [128, 512], BF16)
        NSLOT = 3
        TPP = PIECE // 128
        ringK = [sb(f"ringK{i}", [128, PIECE], BF16) for i in range(NSLOT)]
        ringV = [sb(f"ringV{i}", [128, TPP, 128], BF16) for i in range(NSLOT)]
        PT = [sb(f"PT{i}", [128, 1024], BF16) for i in range(4)]
        eA = sb("eA", [128, 512]); eB = sb("eB", [128, 512]); eD = sb("eD", [128, 512])
        rl = sb("rl", [128, 4, 2]); ss4 = sb("ss4", [128, 4]); rr4 = sb("rr4", [128, 4])

        P.dma(ident, DR(ident_d, "ident_d"))
        P.copy(identb, ident)
        P.memset(ones_bf, 1.0); P.memset(onesf, 1.0); P.memset(mhalf, -0.5)
        P.dma(qkw.re("p a d -> p (a d)"), DR(qkw_d.partition_broadcast(128), "qkw_d"))
        P.dma(lamt.re("p a b d -> p (a b d)"), DR(lam_d.partition_broadcast(128), "lam_d"))
        P.dma(subw, DR(subw_d.partition_broadcast(128), "subw_d"))
        P.dma(sgu.re("p a d -> p (a d)"), DR(sgunw_d.partition_broadcast(128), "sgunw_d"))
        P.dma(wspf, DR(wspT_d, "wspT_d")); P.dma(bsp, DR(bspT_d, "bspT_d"))
        P.dma(cT, DR(cT_d, "cT_d")); P.dma(badaT, DR(badaT_d, "badaT_d")); P.dma(normwT, DR(normwT_d, "normwT_d"))
        P.copy(wsp, wspf)
        for a, scl in ((0, 1.0), (1, 8.0)):
            P.ts(wtab[:, a, 0, :], qkw[:, a, :], scl, None, op0=ALU.mult)
            P.ts(wtab[:, a, 1, 0:32], qkw[:, a, 32:64], scl, None, op0=ALU.mult)
            P.ts(wtab[:, a, 1, 32:64], qkw[:, a, 0:32], scl, None, op0=ALU.mult)
        P.ts(t1[:, 0:128], qkw.re("p a d -> p (a d)"), -1.0, None, op0=ALU.mult)
        P.tt(t1[:, 0:128], t1[:, 0:128], qkw.re("p a d -> p (a d)"), ALU.max)
        P.red(lsm[:, 0:2], t1[:, 0:128].re("p (a d) -> p a d", a=2), op=ALU.max)
        P.tt(lsm[:, 2:3], lsm[:, 0:1], lsm[:, 1:2], ALU.mult)
        P.ts(negB, lsm[:, 2:3], -8.0, None, op0=ALU.mult)
        P.tt(t2[:, 0:128].re("p (a d) -> p a d", a=2), lamt[:, :, 0, :], lamt[:, :, 1, :], ALU.mult)
        P.red(lsm[:, 4:6], t2[:, 0:128].re("p (a d) -> p a d", a=2), op=ALU.add)
        P.actf(lsm[:, 6:8], lsm[:, 4:6], AF.Exp)
        P.tt(lsm[:, 3:4], lsm[:, 7:8], lsm[:, 6:7], ALU.subtract)
        P.ts(neglam, lsm[:, 3:4], -LAM0, None, op0=ALU.add)
        P.ts(subw, subw, float(np.sqrt(128.0) * (1.0 - LAM0) * 0.5), None, op0=ALU.mult)
        P.actf(siluc, cT, AF.Tanh, scale=0.5)
        P.stt(siluc, siluc, 1.0, cT, ALU.add, ALU.mult)
        P.ts(siluc, siluc, 0.5, None, op0=ALU.mult)
        for k in range(8):
            stg = xs[k % 2].re("p a d -> p (a d)")[:, 0:3 * D]
            P.dma(stg, DR(wada_d[k * 128:(k + 1) * 128, :], "wada_d"), partial=False)
            for c6 in range(6):
                P.mm(bank[c6][0:2, :], siluc[:, k, :], stg[:, c6 * 512:(c6 + 1) * 512], start=(k == 0), stop=(k == 7))
        mrow = xs[0].re("p a d -> p (a d)")
        for c6 in range(6):
            P.copy(mrow[0:2, c6 * 512:(c6 + 1) * 512], bank[c6][0:2, :], eng=("dve" if c6 % 2 == 0 else "act"))
        for e in range(24):
            P.tr(bank[7][:, e * 2:(e + 1) * 2], mrow[0:2, e * 128:(e + 1) * 128], ident[0:2, 0:2])
        P.tt(modT, bank[7][:, 0:48].re("p (e g) -> p e g", g=2), V(badaT.ap.unsqueeze(2).broadcast_to([128, 24, 2]), badaT.res), ALU.add)
        nwb = V(normwT.ap.unsqueeze(2).broadcast_to([128, 8, 2]), normwT.res)
        P.stt(gT, modT[:, 8:16, :], 1.0, nwb, ALU.add, ALU.mult)
        P.ts(gT, gT, 32.0, None, op0=ALU.mult)
        shT = modT[:, 0:8, :]
        for hf in range(2):
            P.dma(xs[hf], DR(win_d[hf * 512:(hf + 1) * 512, 512:1536].rearrange("(k p) c -> p k c", p=128), "win_d"), partial=False)
            P.copy(Wkv[:, hf * 4:(hf + 1) * 4, :], xs[hf], eng=("dve" if hf == 0 else "act"))

        cnt = {"x": 0, "ev": 0}

        def bc(v, shape, axis):
            return V(v.ap.unsqueeze(axis).broadcast_to(shape), v.res)

        def rms_part1a(xsb, rp, ABt, g, qk_idx):
            P.tt(ABt, rp, bc(wtab[:, qk_idx], [128, 4, 2, 64], 1), ALU.mult)
            for tt in range(4):
                P.actf(xn[:, tt, :], xsb[:, tt, :], AF.Square, accum=st4[:, tt:tt + 1])
            P.ts(st4, st4, float(D * EPS), None, op0=ALU.add)
            P.tt(rs4, st4, mhalf[:, 0:4], ALU.pow, eng="pool")

        def rms_part1b(xsb):
            for tt in range(4):
                P.actf(xn[:, tt, :], xsb[:, tt, :], AF.Identity, scale=rs4[:, tt:tt + 1])

        def rms_part1(xsb, rp, ABt, g, qk_idx):
            rms_part1a(xsb, rp, ABt, g, qk_idx)
            rms_part1b(xsb)

        def rms_part2(hT, g, ks=range(8), xnv=None, act_share=4):
            xnv = xn if xnv is None else xnv
            for k in ks:
                b = nb("h")
                for tt in range(4):
                    P.tr(bankb[b][:, tt * 128:(tt + 1) * 128], xnv[:, tt, k * 128:(k + 1) * 128], identb)
                cnt["ev"] += 1
                if cnt["ev"] % act_share != 0:
                    P.actf(hT[:, k, :], bankb[b][:, 0:512], AF.Identity, bias=shT[:, k, g:g + 1], scale=gT[:, k, g:g + 1])
                else:
                    P.ts(hT[:, k, :], bankb[b][:, 0:512], gT[:, k, g:g + 1], shT[:, k, g:g + 1], op0=ALU.mult, op1=ALU.add)

        def proj(hT, tt, W, c0):
            b = nb("p")
            for k in range(8):
                P.mm(bank[b], hT[:, k, tt * 128:(tt + 1) * 128], W[:, k, c0:c0 + 512], start=(k == 0), stop=(k == 7))
            return b

        hn = [0]

        def rope_front(b, ABt, t1b):
            i = hn[0] % 2
            out = krs[hn[0] % 3]
            hn[0] += 1
            sq_, st8, rk8 = sqs[i], st8s[i], rk8s[i]
            ps3 = bank[b].re("p (m d) -> p m d", m=8)
            P.actf(sq_, bank[b], AF.Square)
            P.red(st8, sq_.re("p (m d) -> p m d", m=8), op=ALU.add)
            P.ts(st8, st8, float(64 * EPS), None, op0=ALU.add)
            P.tt(rk8, st8, mhalf[:, 0:8], ALU.pow, eng="pool")
            t13 = t1b.re("p (m d) -> p m d", m=8); t23 = t2.re("p (m d) -> p m d", m=8)
            A_ = V(ABt.ap[:, 0:1, :].broadcast_to([128, 8, 64]), ABt.res)
            P.tt(t13, ps3, A_, ALU.mult)
            Bl = V(ABt.ap[:, 1:2, 0:32].broadcast_to([128, 8, 32]), ABt.res)
            Bh = V(ABt.ap[:, 1:2, 32:64].broadcast_to([128, 8, 32]), ABt.res)
            P.tt(t23[:, :, 0:32], ps3[:, :, 32:64], Bl, ALU.mult)
            P.tt(t23[:, :, 32:64], ps3[:, :, 0:32], Bh, ALU.mult)
            P.tt(t1b, t1b, t2, ALU.add)
            return {"t13": t13, "rk8": rk8, "out": out}

        def rope_back(ctx):
            P.tt(ctx["out"].re("p (m d) -> p m d", m=8), ctx["t13"], bc(ctx["rk8"], [128, 8, 64], 2), ALU.mult)

        sch = {"cur": 0, "seq": 0, "q": []}

        def later(delay, fn):
            sch["seq"] += 1
            sch["q"].append((sch["cur"] + delay, sch["seq"], fn))

        def run_due(all_=False):
            while True:
                due = [e for e in sch["q"] if all_ or e[0] <= sch["cur"]]
                if not due:
                    break
                e = min(due)
                sch["q"].remove(e)
                e[2]()

        def step(imm):
            sch["cur"] += 1
            run_due()
            if imm is not None:
                imm()

        def flush():
            run_due(all_=True)

        sts = [(g, st) for g in range(2) for st in range(NK[g] // 512)]

        def A_load(i):
            g, st = sts[i]
            P.dma(xs[i % 2], DR(xkv_d[g][st * 512:(st + 1) * 512, :].rearrange("(t p) d -> p t d", p=128), "xkv"), partial=False)
            P.dma(ropes[i % 2], DR(ropek_d[st * 512:(st + 1) * 512].rearrange("(t p) c d -> p t c d", p=128), "ropek"), partial=False)

        def A_partB(i, tt):
            g, st = sts[i]
            kts = KTs[i % 2]; vs = Vs[i % 2]

            def f(krt):
                b = nb()
                for h in range(4):
                    P.tr(bankb[b][:, h * 128:(h + 1) * 128], krt[:, h * 128:(h + 1) * 128], identb)
                P.copy(kts[:, :, tt * 128:(tt + 1) * 128], bankb[b][:, 0:512].re("p (h n) -> p h n", h=4), eng="act")
                if tt == 3:
                    pc = (st * 512) // PIECE
                    P.dma(DR(kt_scr[g][:, :, st * 512:(st + 1) * 512].rearrange("h p n -> p h n"), f"kts{g}_{pc}"), kts, eng="pool")
                    P.dma(DR(v_scr[g][:, :, st * 4:(st + 1) * 4, :].rearrange("h p t v -> p h (t v)"), f"vs{g}_{pc}"),
                          vs.re("p h t v -> p h (t v)"), eng="pool")
            return f

        junk = za.re("p a d -> p (a d)")[:, 0:1024]
        st4s = [st4, st4b]; rs4s = [rs4, rs4b]
        nst = len(sts)

        def p1a_tile(i, tt):
            P.actf(junk, xs[i % 2][:, tt, :], AF.Square, accum=st4s[i % 2][:, tt:tt + 1])

        def p1a_fin(i):
            P.ts(st4s[i % 2], st4s[i % 2], float(D * EPS), None, op0=ALU.add)
            P.tt(rs4s[i % 2], st4s[i % 2], mhalf[:, 0:4], ALU.pow, eng="pool")

        xnb = [xn, catT.re("p k n -> p (k n)").re("p (t d) -> p t d", t=4)]

        def p1b_tile(i, tt):
            P.actf(xnb[i % 2][:, tt, :], xs[i % 2][:, tt, :], AF.Identity, scale=rs4s[i % 2][:, tt:tt + 1])

        def AB_op(i):
            P.tt(ABs[i % 2], ropes[i % 2], bc(wtab[:, 1], [128, 4, 2, 64], 1), ALU.mult)

        def p1a_all(i):
            for tt in range(4):
                p1a_tile(i, tt)
            p1a_fin(i)

        A_load(0)
        if nst > 1:
            A_load(1)
        AB_op(0)
        p1a_all(0)
        for tt in range(4):
            p1b_tile(0, tt)
        rms_part2(hTs[0], sts[0][0], xnv=xnb[0], act_share=2)
        if nst > 1:
            p1a_all(1)
            for tt in range(4):
                p1b_tile(1, tt)
        if nst > 2:
            A_load(2)
            p1a_all(2)
        for i, (g, st) in enumerate(sts):
            hT = hTs[i % 2]
            if i + 1 < nst:
                AB_op(i + 1)
            if i + 3 < nst:
                A_load(i + 3)
            for tt in range(4):
                bk = proj(hT, tt, Wkv, 0)
                bv = proj(hT, tt, Wkv, 512)
                def a_imm(bk=bk, i=i, tt=tt):
                    ctx = rope_front(bk, ABs[i % 2][:, tt], (t1 if (4 * i + tt) % 2 == 0 else th))
                    later(1, lambda: rope_back(ctx))
                    pb = A_partB(i, tt)
                    later(2, lambda: pb(ctx["out"]))
                step(a_imm)
                P.copy(Vs[i % 2][:, :, tt, :], bank[bv].re("p (h v) -> p h v", h=4), eng="act")
                if i + 1 < nst:
                    rms_part2(hTs[(i + 1) % 2], sts[i + 1][0], ks=((0, 1, 2), (3, 4, 5), (6, 7), ())[tt], xnv=xnb[(i + 1) % 2], act_share=2)
                if i + 2 < nst:
                    p1b_tile(i + 2, tt)
                if i + 3 < nst and tt >= 1:
                    p1a_tile(i + 3, tt - 1)
            if i + 3 < nst:
                p1a_tile(i + 3, 3)
                p1a_fin(i + 3)
        flush()

        P.barrier()
        sets.update({"p": [0, 1, 2, 3], "a": [4, 5], "h": [6, 7]})
        wo_stg = [o_store.re("p t d -> p (t d)").re("p (k c) -> p k c", k=2),
                  V(catT.ap.rearrange("p k n -> p (k n)").bitcast(F32).rearrange("p (k c) -> p k c", k=2), catT.res)]
        gcols = [(g_, k_) for g_ in range(2) for k_ in range(8)]
        for k in range(8):
            stg = xs[k % 2].re("p a d -> p (a d)")
            P.dma(stg[:, 0:512], DR(win_d[k * 128:(k + 1) * 128, 0:512], "win_d2"))
            P.dma(stg[:, 512:2560], DR(win_d[k * 128:(k + 1) * 128, 1536:3584], "win_d3"), partial=True)
            if k < 4:
                P.dma(wo_stg[k % 2], DR(wout_d[k * 256:(k + 1) * 256, :].rearrange("(k p) c -> p k c", p=128), "wout_d"), partial=False)
            P.copy(Wqr[:, k, :], stg[:, 0:2560], eng=("dve" if k % 2 == 0 else "act"))
            if k < 4:
                P.copy(Wout[:, 2 * k:2 * k + 2, :], wo_stg[k % 2], eng=("act" if k % 2 == 0 else "dve"))
            for g_, k_ in gcols[2 * k:2 * k + 2]:
                tb_ = (t1 if k_ % 2 == 0 else t2)[:, 0:128]
                P.ts(tb_, onesf, modT[:, 16 + k_, g_:g_ + 1], None, op0=ALU.mult)
                b = nb()
                P.tr(bank[b][:, 0:128], tb_, ident)
                P.copy(gate_bc[:, g_, k_ * 128:(k_ + 1) * 128], bank[b][:, 0:128], eng="act")

        pcnt = [0]
        chunks = [(g, c) for g in range(2) for c in range(NQ[g] // 512)]

        def B_load(ci):
            g, c = chunks[ci]
            P.dma(xs[ci % 2], DR(xq_d[g][c * 512:(c + 1) * 512, :].rearrange("(t p) d -> p t d", p=128), "xq"), partial=False)
            P.dma(ropes[ci % 2], DR(ropeq_d[g][c * 512:(c + 1) * 512].rearrange("(t p) c d -> p t c d", p=128), "ropeq"), partial=False)

        def q_B(tt):
            def f(krt):
                b = nb()
                for h in range(4):
                    P.tr(bankb[b][:, h * 128:(h + 1) * 128], krt[:, h * 128:(h + 1) * 128], identb)
                P.copy(QT[:, :, tt * 128:(tt + 1) * 128], bankb[b][:, 0:512].re("p (h n) -> p h n", h=4))
            return f

        def za_imm(tt, bz):
            P.actf(th, bank[bz], AF.Tanh, scale=0.5)
            P.stt(za[:, tt, :], th, 1.0, bank[bz], ALU.add, ALU.mult)

        def vg_imm(bg):
            P.red(lns[:, 0:1], bank[bg], op=ALU.add)
            P.actf(sqs[0], bank[bg], AF.Square, accum=lns[:, 1:2])
            P.ts(lns[:, 2:3], lns[:, 0:1], 1.0 / 512, None, op0=ALU.mult)
            P.tt(lns[:, 3:4], lns[:, 2:3], lns[:, 2:3], ALU.mult)
            P.ts(lns[:, 4:5], lns[:, 1:2], 1.0 / 512, float(EPS), op0=ALU.mult, op1=ALU.add)
            P.tt(lns[:, 4:5], lns[:, 4:5], lns[:, 3:4], ALU.subtract)
            P.tt(lns[:, 5:6], lns[:, 4:5], mhalf[:, 0:1], ALU.pow, eng="pool")

        def vg_norm(bg):
            P.stt(th, bank[bg], lns[:, 2:3], sgu[:, 0, :], ALU.subtract, ALU.mult)
            P.stt(vnb, th, lns[:, 5:6], sgu[:, 1, :], ALU.mult, ALU.add)

        def vg_B():
            bm = nb()
            for gg in range(4):
                P.mm(bank[bm][:, gg * 128:(gg + 1) * 128], wsp[:, gg, :], vnb[:, gg * 128:(gg + 1) * 128])
            for gg in range(4):
                P.stt(u_sb[:, gg * 128:(gg + 1) * 128], bank[bm][:, gg * 128:(gg + 1) * 128], bsp[:, gg:gg + 1],
                      u_sb[:, gg * 128:(gg + 1) * 128], ALU.add, ALU.mult)

        zsp = wspf.re("p a d -> p (a d)")

        def zs_imm(bs):
            P.actf(th, bank[bs], AF.Tanh, scale=0.5)
            P.stt(zsp, th, 1.0, bank[bs], ALU.add, ALU.mult)

        def zs_mul():
            P.tt(s_bf, zsp, u_sb, ALU.mult)

        def zs_B(tt):
            def f():
                b = nb()
                for j in range(4):
                    P.tr(bankb[b][:, j * 128:(j + 1) * 128], s_bf[:, j * 128:(j + 1) * 128], identb)
                P.copy(catT[:, 4:8, tt * 128:(tt + 1) * 128], bankb[b][:, 0:512].re("p (h n) -> p h n", h=4), eng="act")
            return f

        pending_inject = []

        def inject():
            if pending_inject:
                pending_inject.pop(0)()

        B_load(0)
        for ci, (g, c) in enumerate(chunks):
                npieces = NK[g] // PIECE
                xsb = xs[ci % 2]
                hT = hTs[ci % 2]
                if ci == 0:
                    rms_part1(xsb, ropes[ci % 2], ABs[ci % 2], g, 0)
                    rms_part2(hT, g)
                for tt in range(4):
                    bg = proj(hT, tt, Wqr, 1536)

                    def vg_step(bg=bg):
                        vg_imm(bg)
                        later(1, lambda: vg_norm(bg))
                        later(4, vg_B)
                    step(vg_step)
                    inject()
                    bq = proj(hT, tt, Wqr, 0)

                    def q_step(bq=bq, tt=tt):
                        ctx = rope_front(bq, ABs[ci % 2][:, tt], t1)
                        later(1, lambda: rope_back(ctx))
                        qb = q_B(tt)
                        later(3, lambda: qb(ctx["out"]))
                    step(q_step)
                    inject()
                    bz = proj(hT, tt, Wqr, 512)
                    step(lambda bz=bz, tt=tt: za_imm(tt, bz))
                    inject()
                    bu = proj(hT, tt, Wqr, 1024)
                    step(lambda bu=bu: P.actf(u_sb, bank[bu], AF.Identity, scale=0.5))
                    inject()
                    bs = proj(hT, tt, Wqr, 2048)

                    def zs_step(bs=bs, tt=tt):
                        zs_imm(bs)
                        later(1, zs_mul)
                        later(3, zs_B(tt))
                    step(zs_step)
                    inject()
                flush()
                while pending_inject:
                    inject()
                if ci + 1 < len(chunks):
                    B_load(ci + 1)

                accA, accB, den, misc = bank[4], bank[5], bank[6], bank[7]
                pieces = [(h, pc) for h in range(4) for pc in range(npieces)]
                slot_of = {}

                def load_piece(j):
                    h, pc = pieces[j]
                    s = pcnt[0] % NSLOT
                    pcnt[0] += 1
                    slot_of[j] = s
                    P.dma(ringK[s], DR(kt_scr[g][h, :, pc * PIECE:(pc + 1) * PIECE], f"kts{g}_{pc}r"), partial=False)
                    P.dma(ringV[s], DR(v_scr[g][h, :, pc * TPP:(pc + 1) * TPP, :], f"vs{g}_{pc}r"), partial=False)

                iters = [(j, t) for j in range(len(pieces)) for t in range(TPP)]
                PF = 2
                for j in range(min(PF, len(pieces))):
                    load_piece(j)

                def qk(i):
                    j, t = iters[i]
                    h = pieces[j][0]
                    s = slot_of[j]
                    S = Sps[i % 2]
                    P.mm(S[:, 0:512], ringK[s][0:64, t * 128:(t + 1) * 128], QT[0:64, h, :], tp=(0, 0))
                    P.mm(S[:, 512:1024], ringK[s][64:128, t * 128:(t + 1) * 128], QT[64:128, h, :], tp=(64, 0))

                qk(0)
                if len(iters) > 1:
                    qk(1)
                for i, (j, t) in enumerate(iters):
                    h, pc = pieces[j]
                    if t == 0 and j + PF < len(pieces):
                        load_piece(j + PF)
                    s = slot_of[j]
                    pt = PT[i % 4]
                    P.actf(pt, Sps[i % 2], AF.Exp, bias=negB[:, 0:1], scale=1.0)
                    if i + 2 < len(iters):
                        qk(i + 2)
                    first = (pc == 0 and t == 0)
                    last = (pc == npieces - 1 and t == TPP - 1)
                    P.mm(accA, ringV[s][:, t, :], pt[:, 0:512], start=first, stop=last)
                    P.mm(accB, ringV[s][:, t, :], pt[:, 512:1024], start=first, stop=last)
                    if t % 2 == 1:
                        pp_ = PT[(i - 1) % 4]
                        dfirst = (pc == 0 and t == 1)
                        P.mm(den[0:32, :], ones_bf, pp_[:, 0:512], start=dfirst, stop=last, tp=(0, 0))
                        P.mm(den[32:64, :], ones_bf, pp_[:, 512:1024], start=dfirst, stop=last, tp=(0, 32))
                        P.mm(den[64:96, :], ones_bf, pt[:, 0:512], start=dfirst, stop=last, tp=(0, 64))
                        P.mm(den[96:128, :], ones_bf, pt[:, 512:1024], start=dfirst, stop=last, tp=(0, 96))
                    if h == 3 and pc == 0 and t == 0 and ci + 1 < len(chunks):
                        rms_part1a(xs[(ci + 1) % 2], ropes[(ci + 1) % 2], ABs[(ci + 1) % 2], chunks[ci + 1][0], 0)
                    if h == 3 and pc == 0 and t == 4 and ci + 1 < len(chunks):
                        for tt_ in range(4):
                            P.ts(xn[:, tt_, :], xs[(ci + 1) % 2][:, tt_, :], rs4[:, tt_:tt_ + 1], None, op0=ALU.mult)
                    if last:
                        P.copy(eA, accA); P.copy(eB, accB, eng="act"); P.copy(eD, den)
                        for qb in range(4):
                            P.tr(misc[:, qb * 128:(qb + 1) * 128], eD[:, qb * 128:(qb + 1) * 128], ident)
                        for qb in range(4):
                            P.tr(accA[:, qb * 128:(qb + 1) * 128], eA[:, qb * 128:(qb + 1) * 128], ident)
                        for qb in range(4):
                            P.tr(accB[:, qb * 128:(qb + 1) * 128], eB[:, qb * 128:(qb + 1) * 128], ident)
                        dT2 = misc.re("p (q h c) -> p q c h", q=4, h=2)
                        P.red(rl[:, :, 0], dT2[:, :, 0, :], op=ALU.add)
                        P.red(rl[:, :, 1], dT2[:, :, 32, :], op=ALU.add)
                        P.recip(rl, rl)
                        P.ts(rl[:, :, 1:2], rl[:, :, 1:2], neglam[:, 0:1], None, op0=ALU.mult)
                        e3 = eA.re("p (q v) -> p q v", q=4); f3 = eB.re("p (q v) -> p q v", q=4)
                        P.tt(e3, accA.re("p (q v) -> p q v", q=4), V(rl.ap[:, :, 0:1].broadcast_to([128, 4, 128]), rl.res), ALU.mult)
                        P.tt(f3, accB.re("p (q v) -> p q v", q=4), V(rl.ap[:, :, 1:2].broadcast_to([128, 4, 128]), rl.res), ALU.mult)
                        P.tt(eA, eA, eB, ALU.add)
                        P.tt(eB, eA, eA, ALU.mult)
                        P.red(ss4, f3, op=ALU.add)
                        P.ts(ss4, ss4, float(128 * EPS), None, op0=ALU.add)
                        P.tt(rr4, ss4, mhalf[:, 0:4], ALU.pow, eng="pool")
                        P.tt(e3, e3, bc(rr4, [128, 4, 128], 2), ALU.mult)
                        P.tt(o_store[:, :, h * 128:(h + 1) * 128], e3, bc(subw, [128, 4, 128], 1), ALU.mult)
                        if h == 3 and ci + 1 < len(chunks):
                            sets["h"] = [0, 1, 2, 3]
                            rms_part2(hTs[(ci + 1) % 2], chunks[ci + 1][0])
                            sets["h"] = [6, 7]

                for tt in range(4):
                    kr = krs[tt % 2]
                    P.tt(kr, o_store[:, tt, :], za[:, tt, :], ALU.mult)
                    b = nb()
                    for j in range(4):
                        P.tr(bankb[b][:, j * 128:(j + 1) * 128], kr[:, j * 128:(j + 1) * 128], identb)
                    P.copy(catT[:, 0:4, tt * 128:(tt + 1) * 128], bankb[b][:, 0:512].re("p (h n) -> p h n", h=4), eng="act")
                inj = []
                for tt in range(4):
                    for hf in range(2):
                        def grp(tt=tt, hf=hf, g=g, xsb=xsb):
                            b = nb("p")
                            for k in range(8):
                                P.mm(bank[b], catT[:, k, tt * 128:(tt + 1) * 128], Wout[:, k, hf * 512:(hf + 1) * 512], start=(k == 0), stop=(k == 7))
                            tb = eA if hf == 0 else eB
                            P.tt(tb, bank[b], gate_bc[:, g, hf * 512:(hf + 1) * 512], ALU.mult)
                            P.tt(xsb[:, tt, hf * 512:(hf + 1) * 512], xsb[:, tt, hf * 512:(hf + 1) * 512], tb, ALU.add, eng="pool")
                        inj.append(grp)

                def ydma(g=g, c=c, xsb=xsb):
                    P.dma(DR(y_d[g][c * 512:(c + 1) * 512, :].rearrange("(t p) d -> p t d", p=128), "y"), xsb, eng="pool")
                inj.append(ydma)
                if ci + 1 < len(chunks):
                    pending_inject.extend(inj)
                else:
                    for f_ in inj:
                        f_()
        P.emit()
    return nc


def rope_tables(S):
    half = 32
    inv_freq = (1.0 / (np.float32(10000.0) ** (np.arange(half, dtype=np.float32) / np.float32(half)))).astype(np.float32)
    ang = (np.arange(S, dtype=np.float32)[:, None] * inv_freq[None, :]).astype(np.float32)
    ang = np.concatenate([ang, ang], axis=-1)
    cos = np.cos(ang.astype(np.float64)).astype(np.float32)
    sin = np.sin(ang.astype(np.float64)).astype(np.float32)
    sin[:, :half] *= -1.0
    return np.ascontiguousarray(np.stack([cos, sin], axis=1))


def prep_inputs(SP, SS, x_prompt, x_sample, c_prompt, c_sample, norm_w, w_ada, b_ada, w_in, w_out,
                q_norm_w, k_norm_w, lambda_qk, subln_w, sgu_norm_w, sgu_norm_b, w_spatial, b_spatial):
    f = lambda a: np.ascontiguousarray(np.asarray(a, dtype=np.float32))
    NQP, NQS = SP // 8, SS // 2
    rope = rope_tables(max(SP, SS))
    common = {
        "rope_k": rope,
        "w_ada": f(w_ada[0]), "b_adaT": f(np.asarray(b_ada[0]).reshape(24, 128).T),
        "norm_wT": f(np.asarray(norm_w[0]).reshape(8, 128).T),
        "w_in": f(w_in[0]), "w_out": f(w_out[0]),
        "qkw": f(np.concatenate([np.asarray(q_norm_w[0]), np.asarray(k_norm_w[0])])),
        "lamqk": f(np.asarray(lambda_qk[0]).reshape(-1)),
        "subw": f(subln_w[0]),
        "sgunw": f(np.concatenate([np.asarray(sgu_norm_w[0]), np.asarray(sgu_norm_b[0])])),
        "wspT": f(np.asarray(w_spatial[0]).transpose(2, 0, 1)),
        "bspT": f(np.asarray(b_spatial[0]).T),
        "ident": np.eye(128, dtype=np.float32),
        "xkv_p": f(x_prompt[0]),
    }
    maps = []
    for c in range(N_CORES):
        b, hf = c // 2, c % 2
        m = dict(common)
        m["xkv_s"] = f(x_sample[b])
        m["xq_p"] = f(x_prompt[0][c * NQP:(c + 1) * NQP])
        m["xq_s"] = f(x_sample[b][hf * NQS:(hf + 1) * NQS])
        m["rope_qp"] = np.ascontiguousarray(rope[c * NQP:(c + 1) * NQP])
        m["rope_qs"] = np.ascontiguousarray(rope[hf * NQS:(hf + 1) * NQS])
        cc = np.stack([np.asarray(c_prompt[0]), np.asarray(c_sample[b])], axis=-1)
        m["cT"] = f(cc.reshape(8, 128, 2).transpose(1, 0, 2))
        maps.append(m)
    return maps


_NC_CACHE = {}


def run(SP, SS, **inputs):
    key = (SP, SS)
    if key not in _NC_CACHE:
        _NC_CACHE[key] = build(SP, SS)
    nc = _NC_CACHE[key]
    maps = prep_inputs(SP, SS, **inputs)
    res = run_bass_kernel_spmd(nc, maps, core_ids=list(range(N_CORES)))
    NQP, NQS = SP // 8, SS // 2
    y_p = np.concatenate([res.results[c]["y_p"] for c in range(N_CORES)], axis=0)[None]
    y_s = np.stack([np.concatenate([res.results[2 * b]["y_s"], res.results[2 * b + 1]["y_s"]], axis=0) for b in range(4)], axis=0)
    return y_p.astype(np.float32), y_s.astype(np.float32)


def kernel(**inputs):
    inputs = {k: np.asarray(v) for k, v in inputs.items()}
    return run(16384, 8192, **inputs)
```

```python
import numpy as np
import concourse.bass as bass
import concourse.mybir as mybir

F32 = mybir.dt.float32
BF16 = mybir.dt.bfloat16
ALU = mybir.AluOpType
AF = mybir.ActivationFunctionType
AX = mybir.AxisListType

COMPUTE = ("pe", "act", "dve", "pool")


class Res:
    __slots__ = ("name", "writers", "readers", "prev_readers")

    def __init__(self, name):
        self.name = name
        self.writers = []
        self.readers = []
        self.prev_readers = []


class V:
    __slots__ = ("ap", "res")

    def __init__(self, ap, res):
        self.ap = ap
        self.res = res if isinstance(res, (list, tuple)) else [res]

    def __getitem__(self, k):
        return V(self.ap[k], self.res)

    def re(self, s, **kw):
        return V(self.ap.rearrange(s, **kw), self.res)


class Op:
    __slots__ = ("eng", "fn", "deps", "dma", "signal", "count", "sem", "idx", "nm", "partial")


class Prog:
    def __init__(self, nc, n_dma_sems=20):
        self.nc = nc
        self.ops = []
        self.n_dma_sems = n_dma_sems
        self.last = {}
        self.pending = {}
        self.dmas_since_barrier = []

    def add(self, eng, fn, reads=(), writes=(), dma=False, partial=False, nm=""):
        op = Op()
        op.eng, op.fn, op.dma, op.nm = eng, fn, dma, nm
        op.idx = len(self.ops)
        op.signal = False
        op.count = None
        op.sem = None
        op.partial = partial
        deps = set()
        rres, wres = [], []
        for v in reads:
            for r in (v.res if isinstance(v, V) else [v]):
                if r not in rres:
                    rres.append(r)
        for v in writes:
            for r in (v.res if isinstance(v, V) else [v]):
                if r not in wres:
                    wres.append(r)
        for r in rres:
            deps.update(r.writers)
        for r in wres:
            deps.update(r.readers)
            deps.update(r.prev_readers)
            for w in r.writers:
                if not (partial and w.partial):
                    deps.add(w)
            if r.readers:
                r.prev_readers = r.readers
                r.readers = []
                r.writers = [op]
            else:
                if partial:
                    r.writers.append(op)
                else:
                    r.writers = [op]
                    r.prev_readers = []
        for r in rres:
            r.readers.append(op)
        if self.pending.get(eng):
            deps.update(self.pending.pop(eng))
        deps.discard(op)
        if dma:
            self.dmas_since_barrier.append(op)
        self.last[eng] = op
        keep = []
        for d in deps:
            if d.eng == eng and eng == "pe" and not d.dma and not dma:
                continue
            keep.append(d)
        op.deps = keep
        self.ops.append(op)
        return op

    def barrier(self):
        allv = list(self.last.values()) + list(self.dmas_since_barrier)
        for e in COMPUTE + ("sp",):
            self.pending[e] = set(allv) | set(self.pending.get(e, ()))
        self.dmas_since_barrier = []

    def emit(self, final_wait_eng="sp"):
        nc = self.nc
        ops = self.ops
        for op in ops:
            for d in op.deps:
                d.signal = True
        for op in ops:
            if op.dma:
                op.signal = True
        cnt = {e: 0 for e in COMPUTE + ("sp",)}
        dma_rr = {}
        dma_sem_cnt = {}
        dma_last_on_sem = {}
        import contextlib
        es = contextlib.ExitStack()
        with es:
            sems = {e: es.enter_context(nc.semaphore("s_" + e)) for e in COMPUTE}
            dsems = {}
            for e in ("sp", "pool", "act"):
                dsems[e] = [es.enter_context(nc.semaphore(f"d_{e}{i}")) for i in range(self.n_dma_sems if e == "sp" else 12)]
                dma_rr[e] = 0
            extra_dma_wait = {}
            for op in ops:
                if op.dma:
                    lst = dsems[op.eng]
                    k = dma_rr[op.eng] % len(lst)
                    dma_rr[op.eng] += 1
                    s = lst[k]
                    dma_sem_cnt[s] = dma_sem_cnt.get(s, 0) + 16
                    op.sem, op.count = s, dma_sem_cnt[s]
                    prev = dma_last_on_sem.get(s)
                    if prev is not None:
                        extra_dma_wait[op.idx] = prev
                    dma_last_on_sem[s] = op
                elif op.signal:
                    cnt[op.eng] += 1
                    op.sem, op.count = sems[op.eng], cnt[op.eng]
            self.max_counts = dict(cnt)
            byeng = {}
            for op in ops:
                byeng.setdefault(op.eng, []).append(op)

            def run(engname, eng):
                waited = {}
                for op in byeng.get(engname, []):
                    need = {}
                    dl = list(op.deps)
                    if op.idx in extra_dma_wait:
                        dl.append(extra_dma_wait[op.idx])
                    for d in dl:
                        if d.sem is None:
                            continue
                        if need.get(d.sem, 0) < d.count:
                            need[d.sem] = d.count
                    for s, c in need.items():
                        if waited.get(s, 0) < c:
                            eng.wait_ge(s, c)
                            waited[s] = c
                    ins = op.fn(eng)
                    if op.sem is not None:
                        ins.then_inc(op.sem, 16 if op.dma else 1)
                if engname == final_wait_eng:
                    for s, c in dma_sem_cnt.items():
                        if waited.get(s, 0) < c:
                            eng.wait_ge(s, c)
                    for e2 in COMPUTE:
                        if cnt[e2] > 0:
                            eng.wait_ge(sems[e2], cnt[e2])

            with nc.Block() as block:
                @block.tensor
                def _(e):
                    run("pe", e)

                @block.scalar
                def _(e):
                    run("act", e)

                @block.vector
                def _(e):
                    run("dve", e)

                @block.gpsimd
                def _(e):
                    run("pool", e)

                @block.sync
                def _(e):
                    run("sp", e)

    def mm(self, out, lhsT, rhs, start=True, stop=True, tp=None, nm="mm"):
        kw = {}
        if tp is not None:
            kw["tile_position"] = tp
        return self.add("pe", lambda e: e.matmul(out.ap, lhsT.ap, rhs.ap, start=start, stop=stop, **kw),
                        reads=[lhsT, rhs], writes=[out], partial=True, nm=nm)

    def tr(self, out, in_, ident, nm="tr"):
        return self.add("pe", lambda e: e.transpose(out.ap, in_.ap, ident.ap),
                        reads=[in_, ident], writes=[out], partial=True, nm=nm)

    def actf(self, out, in_, func, bias=None, scale=1.0, accum=None, eng="act", nm="act"):
        reads = [in_]
        kw = {}
        if bias is not None:
            if isinstance(bias, V):
                reads.append(bias)
                kw["bias"] = bias.ap
            else:
                kw["bias"] = bias
        if isinstance(scale, V):
            reads.append(scale)
            kw["scale"] = scale.ap
        else:
            kw["scale"] = scale
        writes = [out]
        if accum is not None:
            writes.append(accum)
            kw["accum_out"] = accum.ap
        return self.add("act", lambda e: e.activation(out.ap, in_.ap, func, **kw), reads=reads, writes=writes, nm=nm)

    def tt(self, out, in0, in1, op, eng="dve", nm="tt"):
        return self.add(eng, lambda e: e.tensor_tensor(out.ap, in0.ap, in1.ap, op), reads=[in0, in1], writes=[out], nm=nm)

    def ts(self, out, in0, s1, s2=None, op0=ALU.mult, op1=None, eng="dve", accum=None, nm="ts"):
        reads = [in0]
        a1 = s1.ap if isinstance(s1, V) else s1
        a2 = s2.ap if isinstance(s2, V) else s2
        if isinstance(s1, V):
            reads.append(s1)
        if isinstance(s2, V):
            reads.append(s2)
        kw = {}
        if op1 is not None:
            kw["op1"] = op1
        writes = [out]
        if accum is not None:
            writes.append(accum)
            kw["accum_out"] = accum.ap
        return self.add(eng, lambda e: e.tensor_scalar(out.ap, in0.ap, a1, a2, op0, **kw), reads=reads, writes=writes, nm=nm)

    def stt(self, out, in0, scalar, in1, op0, op1, eng="dve", nm="stt"):
        reads = [in0, in1]
        a = scalar.ap if isinstance(scalar, V) else scalar
        if isinstance(scalar, V):
            reads.append(scalar)
        return self.add(eng, lambda e: e.scalar_tensor_tensor(out.ap, in0.ap, a, in1.ap, op0, op1), reads=reads, writes=[out], nm=nm)

    def copy(self, out, in_, eng="dve", nm="copy"):
        if eng == "act":
            return self.add("act", lambda e: e.copy(out.ap, in_.ap), reads=[in_], writes=[out], nm=nm)
        return self.add(eng, lambda e: e.tensor_copy(out.ap, in_.ap), reads=[in_], writes=[out], nm=nm)

    def red(self, out, in_, op=ALU.add, axis=AX.X, eng="dve", nm="red", **kw):
        return self.add(eng, lambda e: e.tensor_reduce(out.ap, in_.ap, axis, op, **kw), reads=[in_], writes=[out], nm=nm)

    def recip(self, out, in_, nm="recip"):
        return self.add("dve", lambda e: e.reciprocal(out.ap, in_.ap), reads=[in_], writes=[out], nm=nm)

    def memset(self, out, val, eng="dve", nm="memset"):
        return self.add(eng, lambda e: e.memset(out.ap, val), writes=[out], nm=nm)

    def dma(self, out, in_, eng="sp", partial=True, nm="dma", **kw):
        return self.add(eng, lambda e: e.dma_start(out=out.ap, in_=in_.ap, **kw), reads=[in_], writes=[out], dma=True,
                        partial=partial, nm=nm)


import contextlib
from concourse.bass_utils import run_bass_kernel_spmd

D = 1024
EPS = 1e-6
LAM0 = 0.8 - 0.6 * 1.0
N_CORES = 8


def build(SP, SS, PIECE=1024):
    NQ = [SP // 8, SS // 2]
    NK = [SP, SS]
    SMAX = max(SP, SS)
    nc = bass.Bass("TRN2", target_bir_lowering=False)
    din = lambda n, shp: nc.dram_tensor(n, shp, F32, kind="ExternalInput").ap()
    xkv_d = [din("xkv_p", [SP, D]), din("xkv_s", [SS, D])]
    xq_d = [din("xq_p", [NQ[0], D]), din("xq_s", [NQ[1], D])]
    ropek_d = din("rope_k", [SMAX, 2, 64])
    ropeq_d = [din("rope_qp", [NQ[0], 2, 64]), din("rope_qs", [NQ[1], 2, 64])]
    cT_d = din("cT", [128, 8, 2])
    wada_d = din("w_ada", [D, 3 * D])
    badaT_d = din("b_adaT", [128, 24])
    normwT_d = din("norm_wT", [128, 8])
    win_d = din("w_in", [D, 3584])
    wout_d = din("w_out", [D, D])
    qkw_d = din("qkw", [128])
    lam_d = din("lamqk", [256])
    subw_d = din("subw", [128])
    sgunw_d = din("sgunw", [1024])
    wspT_d = din("wspT", [128, 4, 128])
    bspT_d = din("bspT", [128, 4])
    ident_d = din("ident", [128, 128])
    y_d = [nc.dram_tensor("y_p", [NQ[0], D], F32, kind="ExternalOutput").ap(),
           nc.dram_tensor("y_s", [NQ[1], D], F32, kind="ExternalOutput").ap()]
    kt_scr = [nc.dram_tensor(f"kt_scr{g}", [4, 128, NK[g]], BF16).ap() for g in range(2)]
    v_scr = [nc.dram_tensor(f"v_scr{g}", [4, 128, NK[g] // 128, 128], BF16).ap() for g in range(2)]

    P = Prog(nc)
    es = contextlib.ExitStack()
    with es:
        def sb(name, shape, dt=F32):
            h = es.enter_context(nc.sbuf_tensor("sb_" + name, shape, dt))
            return V(h[:], Res(name))
        DR = lambda ap, n: V(ap, Res(n))

        pp = [es.enter_context(nc.psum_tensor(f"pp{i}", [128, 1024], F32)) for i in range(4)]
        bres = [Res(f"bank{i}") for i in range(8)]
        bank = [V(pp[b // 2][:, (b % 2) * 512:(b % 2) * 512 + 512], bres[b]) for b in range(8)]
        bankb = [V(pp[b // 2][:, (b % 2) * 512:(b % 2) * 512 + 512].bitcast(BF16), bres[b]) for b in range(8)]
        Sps = [V(pp[0][:, :], [bres[0], bres[1]]), V(pp[1][:, :], [bres[2], bres[3]])]
        rrc = {"p": 0, "a": 0, "h": 0}
        sets = {"p": [0, 1, 2, 3], "a": [4], "h": [5, 6, 7]}

        def nb(kind="a"):
            rrc[kind] += 1
            st_ = sets[kind]
            return st_[rrc[kind] % len(st_)]

        ident = sb("ident", [128, 128]); identb = sb("identb", [128, 128], BF16)
        ones_bf = sb("ones_bf", [128, 32], BF16); onesf = sb("onesf", [128, 128])
        mhalf = sb("mhalf", [128, 16])
        qkw = sb("qkw", [128, 2, 64]); wtab = sb("wtab", [128, 2, 2, 64])
        lamt = sb("lamt", [128, 2, 2, 64]); lsm = sb("lsm", [128, 8])
        negB = sb("negB", [128, 1]); neglam = sb("neglam", [128, 1])
        subw = sb("subw", [128, 128]); sgu = sb("sgu", [128, 2, 512])
        wspf = sb("wspf", [128, 4, 128]); wsp = sb("wsp", [128, 4, 128], BF16); bsp = sb("bsp", [128, 4])
        cT = sb("cT", [128, 8, 2]); siluc = sb("siluc", [128, 8, 2]); badaT = sb("badaT", [128, 24]); normwT = sb("normwT", [128, 8])
        modT = sb("modT", [128, 24, 2]); gT = sb("gT", [128, 8, 2])
        gate_bc = sb("gate_bc", [128, 2, D])
        Wout = sb("Wout", [128, 8, D], BF16)
        xs = [sb(f"xs{i}", [128, 4, D]) for i in range(2)]
        arena_h = es.enter_context(nc.sbuf_tensor("sb_arena", [128, 8 * 2560], BF16))
        ar_res = Res("arena")
        Wqr = V(arena_h[:, :].rearrange("p (k c) -> p k c", k=8), ar_res)
        Wkv = V(arena_h[:, 0:8192].rearrange("p (k c) -> p k c", k=8), Res("Wkv"))
        KTs = [V(arena_h[:, 8192 + i * 2048: 8192 + (i + 1) * 2048].rearrange("p (h n) -> p h n", h=4), Res(f"KTs{i}")) for i in range(2)]
        Vs = [V(arena_h[:, 12288 + i * 2048: 12288 + (i + 1) * 2048].rearrange("p (h t v) -> p h t v", h=4, t=4), Res(f"Vs{i}")) for i in range(2)]
        xn = sb("xn", [128, 4, D], BF16)
        hTs = [sb(f"hT{i}", [128, 8, 512], BF16) for i in range(2)]
        sqs = [sb(f"sq{i}", [128, 512]) for i in range(2)]; sq = sqs[0]; t1 = sb("t1", [128, 512]); t2 = sb("t2", [128, 512])
        krs = [sb(f"kr{i}", [128, 512], BF16) for i in range(3)]; kr = krs[0]
        ropes = [sb(f"ropes{i}", [128, 4, 2, 64]) for i in range(2)]
        ABs = [sb(f"AB{i}", [128, 4, 2, 64]) for i in range(2)]
        st4 = sb("st4", [128, 4]); rs4 = sb("rs4", [128, 4]); st4b = sb("st4b", [128, 4]); rs4b = sb("rs4b", [128, 4]); st8s = [sb(f"st8{i}", [128, 8]) for i in range(2)]; rk8s = [sb(f"rk8{i}", [128, 8]) for i in range(2)]
        lns = sb("lns", [128, 8])
        QT = sb("QT", [128, 4, 512], BF16)
        za = sb("za", [128, 4, 512], BF16)
        catT = sb("catT", [128, 8, 512], BF16)
        s_bf = sb("s_bf", [128, 512], BF16)
        o_store = sb("o_store", [128, 4, 512])
        u_sb = sb("u_sb", [128, 512]); th = sb("th", [128, 512]); vnb = sb("vnb", [128, 512], BF16)
        NSLOT = 3
        TPP = PIECE // 128
        ringK = [sb(f"ringK{i}", [128, PIECE], BF16) for i in range(NSLOT)]
        ringV = [sb(f"ringV{i}", [128, TPP, 128], BF16) for i in range(NSLOT)]
        PT = [sb(f"PT{i}", [128, 1024], BF16) for i in range(4)]
        eA = sb("eA", [128, 512]); eB = sb("eB", [128, 512]); eD = sb("eD", [128, 512])
        rl = sb("rl", [128, 4, 2]); ss4 = sb("ss4", [128, 4]); rr4 = sb("rr4", [128, 4])

        P.dma(ident, DR(ident_d, "ident_d"))
        P.copy(identb, ident)
        P.memset(ones_bf, 1.0); P.memset(onesf, 1.0); P.memset(mhalf, -0.5)
        P.dma(qkw.re("p a d -> p (a d)"), DR(qkw_d.partition_broadcast(128), "qkw_d"))
        P.dma(lamt.re("p a b d -> p (a b d)"), DR(lam_d.partition_broadcast(128), "lam_d"))
        P.dma(subw, DR(subw_d.partition_broadcast(128), "subw_d"))
        P.dma(sgu.re("p a d -> p (a d)"), DR(sgunw_d.partition_broadcast(128), "sgunw_d"))
        P.dma(wspf, DR(wspT_d, "wspT_d")); P.dma(bsp, DR(bspT_d, "bspT_d"))
        P.dma(cT, DR(cT_d, "cT_d")); P.dma(badaT, DR(badaT_d, "badaT_d")); P.dma(normwT, DR(normwT_d, "normwT_d"))
        P.copy(wsp, wspf)
        for a, scl in ((0, 1.0), (1, 8.0)):
            P.ts(wtab[:, a, 0, :], qkw[:, a, :], scl, None, op0=ALU.mult)
            P.ts(wtab[:, a, 1, 0:32], qkw[:, a, 32:64], scl, None, op0=ALU.mult)
            P.ts(wtab[:, a, 1, 32:64], qkw[:, a, 0:32], scl, None, op0=ALU.mult)
        P.ts(t1[:, 0:128], qkw.re("p a d -> p (a d)"), -1.0, None, op0=ALU.mult)
        P.tt(t1[:, 0:128], t1[:, 0:128], qkw.re("p a d -> p (a d)"), ALU.max)
        P.red(lsm[:, 0:2], t1[:, 0:128].re("p (a d) -> p a d", a=2), op=ALU.max)
        P.tt(lsm[:, 2:3], lsm[:, 0:1], lsm[:, 1:2], ALU.mult)
        P.ts(negB, lsm[:, 2:3], -8.0, None, op0=ALU.mult)
        P.tt(t2[:, 0:128].re("p (a d) -> p a d", a=2), lamt[:, :, 0, :], lamt[:, :, 1, :], ALU.mult)
        P.red(lsm[:, 4:6], t2[:, 0:128].re("p (a d) -> p a d", a=2), op=ALU.add)
        P.actf(lsm[:, 6:8], lsm[:, 4:6], AF.Exp)
        P.tt(lsm[:, 3:4], lsm[:, 7:8], lsm[:, 6:7], ALU.subtract)
        P.ts(neglam, lsm[:, 3:4], -LAM0, None, op0=ALU.add)
        P.ts(subw, subw, float(np.sqrt(128.0) * (1.0 - LAM0) * 0.5), None, op0=ALU.mult)
        P.actf(siluc, cT, AF.Tanh, scale=0.5)
        P.stt(siluc, siluc, 1.0, cT, ALU.add, ALU.mult)
        P.ts(siluc, siluc, 0.5, None, op0=ALU.mult)
        for k in range(8):
            stg = xs[k % 2].re("p a d -> p (a d)")[:, 0:3 * D]
            P.dma(stg, DR(wada_d[k * 128:(k + 1) * 128, :], "wada_d"), partial=False)
            for c6 in range(6):
                P.mm(bank[c6][0:2, :], siluc[:, k, :], stg[:, c6 * 512:(c6 + 1) * 512], start=(k == 0), stop=(k == 7))
        mrow = xs[0].re("p a d -> p (a d)")
        for c6 in range(6):
            P.copy(mrow[0:2, c6 * 512:(c6 + 1) * 512], bank[c6][0:2, :], eng=("dve" if c6 % 2 == 0 else "act"))
        for e in range(24):
            P.tr(bank[7][:, e * 2:(e + 1) * 2], mrow[0:2, e * 128:(e + 1) * 128], ident[0:2, 0:2])
        P.tt(modT, bank[7][:, 0:48].re("p (e g) -> p e g", g=2), V(badaT.ap.unsqueeze(2).broadcast_to([128, 24, 2]), badaT.res), ALU.add)
        nwb = V(normwT.ap.unsqueeze(2).broadcast_to([128, 8, 2]), normwT.res)
        P.stt(gT, modT[:, 8:16, :], 1.0, nwb, ALU.add, ALU.mult)
        P.ts(gT, gT, 32.0, None, op0=ALU.mult)
        shT = modT[:, 0:8, :]
        for hf in range(2):
            P.dma(xs[hf], DR(win_d[hf * 512:(hf + 1) * 512, 512:1536].rearrange("(k p) c -> p k c", p=128), "win_d"), partial=False)
            P.copy(Wkv[:, hf * 4:(hf + 1) * 4, :], xs[hf], eng=("dve" if hf == 0 else "act"))

        cnt = {"x": 0, "ev": 0}

        def bc(v, shape, axis):
            return V(v.ap.unsqueeze(axis).broadcast_to(shape), v.res)

        def rms_part1a(xsb, rp, ABt, g, qk_idx):
            P.tt(ABt, rp, bc(wtab[:, qk_idx], [128, 4, 2, 64], 1), ALU.mult)
            for tt in range(4):
                P.actf(xn[:, tt, :], xsb[:, tt, :], AF.Square, accum=st4[:, tt:tt + 1])
            P.ts(st4, st4, float(D * EPS), None, op0=ALU.add)
            P.tt(rs4, st4, mhalf[:, 0:4], ALU.pow, eng="pool")

        def rms_part1b(xsb):
            for tt in range(4):
                P.actf(xn[:, tt, :], xsb[:, tt, :], AF.Identity, scale=rs4[:, tt:tt + 1])

        def rms_part1(xsb, rp, ABt, g, qk_idx):
            rms_part1a(xsb, rp, ABt, g, qk_idx)
            rms_part1b(xsb)

        def rms_part2(hT, g, ks=range(8), xnv=None, act_share=4):
            xnv = xn if xnv is None else xnv
            for k in ks:
                b = nb("h")
                for tt in range(4):
                    P.tr(bankb[b][:, tt * 128:(tt + 1) * 128], xnv[:, tt, k * 128:(k + 1) * 128], identb)
                cnt["ev"] += 1
                if cnt["ev"] % act_share != 0:
                    P.actf(hT[:, k, :], bankb[b][:, 0:512], AF.Identity, bias=shT[:, k, g:g + 1], scale=gT[:, k, g:g + 1])
                else:
                    P.ts(hT[:, k, :], bankb[b][:, 0:512], gT[:, k, g:g + 1], shT[:, k, g:g + 1], op0=ALU.mult, op1=ALU.add)

        def proj(hT, tt, W, c0):
            b = nb("p")
            for k in range(8):
                P.mm(bank[b], hT[:, k, tt * 128:(tt + 1) * 128], W[:, k, c0:c0 + 512], start=(k == 0), stop=(k == 7))
            return b

        hn = [0]

        def rope_front(b, ABt, t1b, add_eng="dve"):
            i = hn[0] % 2
            out = krs[hn[0] % 3]
            hn[0] += 1
            sq_, st8, rk8 = sqs[i], st8s[i], rk8s[i]
            ps3 = bank[b].re("p (m d) -> p m d", m=8)
            P.actf(sq_, bank[b], AF.Square)
            P.red(st8, sq_.re("p (m d) -> p m d", m=8), op=ALU.add)
            P.ts(st8, st8, float(64 * EPS), None, op0=ALU.add)
            P.tt(rk8, st8, mhalf[:, 0:8], ALU.pow, eng="pool")
            t13 = t1b.re("p (m d) -> p m d", m=8); t23 = t2.re("p (m d) -> p m d", m=8)
            A_ = V(ABt.ap[:, 0:1, :].broadcast_to([128, 8, 64]), ABt.res)
            P.tt(t13, ps3, A_, ALU.mult)
            Bl = V(ABt.ap[:, 1:2, 0:32].broadcast_to([128, 8, 32]), ABt.res)
            Bh = V(ABt.ap[:, 1:2, 32:64].broadcast_to([128, 8, 32]), ABt.res)
            P.tt(t23[:, :, 0:32], ps3[:, :, 32:64], Bl, ALU.mult)
            P.tt(t23[:, :, 32:64], ps3[:, :, 0:32], Bh, ALU.mult)
            P.tt(t1b, t1b, t2, ALU.add, eng=add_eng)
            return {"t13": t13, "rk8": rk8, "out": out}

        def rope_back(ctx):
            P.tt(ctx["out"].re("p (m d) -> p m d", m=8), ctx["t13"], bc(ctx["rk8"], [128, 8, 64], 2), ALU.mult)

        sch = {"cur": 0, "seq": 0, "q": []}

        def later(delay, fn):
            sch["seq"] += 1
            sch["q"].append((sch["cur"] + delay, sch["seq"], fn))

        def run_due(all_=False):
            while True:
                due = [e for e in sch["q"] if all_ or e[0] <= sch["cur"]]
                if not due:
                    break
                e = min(due)
                sch["q"].remove(e)
                e[2]()

        def step(imm):
            sch["cur"] += 1
            run_due()
            if imm is not None:
                imm()

        def flush():
            run_due(all_=True)

        sts = [(g, st) for g in range(2) for st in range(NK[g] // 512)]

        def A_load(i):
            g, st = sts[i]
            P.dma(xs[i % 2], DR(xkv_d[g][st * 512:(st + 1) * 512, :].rearrange("(t p) d -> p t d", p=128), "xkv"), partial=False)
            P.dma(ropes[i % 2], DR(ropek_d[st * 512:(st + 1) * 512].rearrange("(t p) c d -> p t c d", p=128), "ropek"), partial=False)

        def A_partB(i, tt):
            g, st = sts[i]
            kts = KTs[i % 2]; vs = Vs[i % 2]

            def f(krt):
                b = nb()
                for h in range(4):
                    P.tr(bankb[b][:, h * 128:(h + 1) * 128], krt[:, h * 128:(h + 1) * 128], identb)
                P.copy(kts[:, :, tt * 128:(tt + 1) * 128], bankb[b][:, 0:512].re("p (h n) -> p h n", h=4))
                if tt == 3:
                    pc = (st * 512) // PIECE
                    P.dma(DR(kt_scr[g][:, :, st * 512:(st + 1) * 512].rearrange("h p n -> p h n"), f"kts{g}_{pc}"), kts, eng="pool")
                    P.dma(DR(v_scr[g][:, :, st * 4:(st + 1) * 4, :].rearrange("h p t v -> p h (t v)"), f"vs{g}_{pc}"),
                          vs.re("p h t v -> p h (t v)"), eng="pool")
            return f

        junk = za.re("p a d -> p (a d)")[:, 0:1024]
        st4s = [st4, st4b]; rs4s = [rs4, rs4b]
        nst = len(sts)

        def p1a_tile(i, tt):
            P.actf(junk, xs[i % 2][:, tt, :], AF.Square, accum=st4s[i % 2][:, tt:tt + 1])

        def p1a_fin(i):
            P.ts(st4s[i % 2], st4s[i % 2], float(D * EPS), None, op0=ALU.add)
            P.tt(rs4s[i % 2], st4s[i % 2], mhalf[:, 0:4], ALU.pow, eng="pool")

        xnb = [xn, catT.re("p k n -> p (k n)").re("p (t d) -> p t d", t=4)]

        def p1b_tile(i, tt):
            P.actf(xnb[i % 2][:, tt, :], xs[i % 2][:, tt, :], AF.Identity, scale=rs4s[i % 2][:, tt:tt + 1])

        def AB_op(i):
            P.tt(ABs[i % 2], ropes[i % 2], bc(wtab[:, 1], [128, 4, 2, 64], 1), ALU.mult)

        def p1a_all(i):
            for tt in range(4):
                p1a_tile(i, tt)
            p1a_fin(i)

        A_load(0)
        if nst > 1:
            A_load(1)
        AB_op(0)
        p1a_all(0)
        for tt in range(4):
            p1b_tile(0, tt)
        rms_part2(hTs[0], sts[0][0], xnv=xnb[0], act_share=2)
        if nst > 1:
            p1a_all(1)
            for tt in range(4):
                p1b_tile(1, tt)
        if nst > 2:
            A_load(2)
            p1a_all(2)
        for i, (g, st) in enumerate(sts):
            hT = hTs[i % 2]
            if i + 1 < nst:
                AB_op(i + 1)
            if i + 3 < nst:
                A_load(i + 3)
            for tt in range(4):
                bk = proj(hT, tt, Wkv, 0)
                bv = proj(hT, tt, Wkv, 512)
                def a_imm(bk=bk, i=i, tt=tt):
                    ctx = rope_front(bk, ABs[i % 2][:, tt], (t1 if (4 * i + tt) % 2 == 0 else th), add_eng="pool")
                    later(1, lambda: rope_back(ctx))
                    pb = A_partB(i, tt)
                    later(2, lambda: pb(ctx["out"]))
                step(a_imm)
                P.copy(Vs[i % 2][:, :, tt, :], bank[bv].re("p (h v) -> p h v", h=4), eng="act")
                if i + 1 < nst:
                    rms_part2(hTs[(i + 1) % 2], sts[i + 1][0], ks=((0, 1, 2), (3, 4, 5), (6, 7), ())[tt], xnv=xnb[(i + 1) % 2], act_share=2)
                if i + 2 < nst:
                    p1b_tile(i + 2, tt)
                if i + 3 < nst and tt >= 1:
                    p1a_tile(i + 3, tt - 1)
            if i + 3 < nst:
                p1a_tile(i + 3, 3)
                p1a_fin(i + 3)
        flush()

        P.barrier()
        sets.update({"p": [0, 1, 2, 3], "a": [4, 5], "h": [6, 7]})
        wo_stg = [o_store.re("p t d -> p (t d)").re("p (k c) -> p k c", k=2),
                  V(catT.ap.rearrange("p k n -> p (k n)").bitcast(F32).rearrange("p (k c) -> p k c", k=2), catT.res)]
        gcols = [(g_, k_) for g_ in range(2) for k_ in range(8)]
        for k in range(8):
            stg = xs[k % 2].re("p a d -> p (a d)")
            P.dma(stg[:, 0:512], DR(win_d[k * 128:(k + 1) * 128, 0:512], "win_d2"))
            P.dma(stg[:, 512:2560], DR(win_d[k * 128:(k + 1) * 128, 1536:3584], "win_d3"), partial=True)
            if k < 4:
                P.dma(wo_stg[k % 2], DR(wout_d[k * 256:(k + 1) * 256, :].rearrange("(k p) c -> p k c", p=128), "wout_d"), partial=False)
            P.copy(Wqr[:, k, :], stg[:, 0:2560], eng=("dve" if k % 2 == 0 else "act"))
            if k < 4:
                P.copy(Wout[:, 2 * k:2 * k + 2, :], wo_stg[k % 2], eng=("act" if k % 2 == 0 else "dve"))
            for g_, k_ in gcols[2 * k:2 * k + 2]:
                tb_ = (t1 if k_ % 2 == 0 else t2)[:, 0:128]
                P.ts(tb_, onesf, modT[:, 16 + k_, g_:g_ + 1], None, op0=ALU.mult)
                b = nb()
                P.tr(bank[b][:, 0:128], tb_, ident)
                P.copy(gate_bc[:, g_, k_ * 128:(k_ + 1) * 128], bank[b][:, 0:128], eng="act")

        pcnt = [0]
        chunks = [(g, c) for g in range(2) for c in range(NQ[g] // 512)]

        def B_load(ci):
            g, c = chunks[ci]
            P.dma(xs[ci % 2], DR(xq_d[g][c * 512:(c + 1) * 512, :].rearrange("(t p) d -> p t d", p=128), "xq"), partial=False)
            P.dma(ropes[ci % 2], DR(ropeq_d[g][c * 512:(c + 1) * 512].rearrange("(t p) c d -> p t c d", p=128), "ropeq"), partial=False)

        def q_B(tt):
            def f(krt):
                b = nb()
                for h in range(4):
                    P.tr(bankb[b][:, h * 128:(h + 1) * 128], krt[:, h * 128:(h + 1) * 128], identb)
                P.copy(QT[:, :, tt * 128:(tt + 1) * 128], bankb[b][:, 0:512].re("p (h n) -> p h n", h=4))
            return f

        def za_imm(tt, bz):
            P.actf(th, bank[bz], AF.Tanh, scale=0.5)
            P.stt(za[:, tt, :], th, 1.0, bank[bz], ALU.add, ALU.mult)

        def vg_imm(bg):
            P.red(lns[:, 0:1], bank[bg], op=ALU.add)
            P.actf(sqs[0], bank[bg], AF.Square, accum=lns[:, 1:2])
            P.ts(lns[:, 2:3], lns[:, 0:1], 1.0 / 512, None, op0=ALU.mult)
            P.tt(lns[:, 3:4], lns[:, 2:3], lns[:, 2:3], ALU.mult)
            P.ts(lns[:, 4:5], lns[:, 1:2], 1.0 / 512, float(EPS), op0=ALU.mult, op1=ALU.add)
            P.tt(lns[:, 4:5], lns[:, 4:5], lns[:, 3:4], ALU.subtract)
            P.tt(lns[:, 5:6], lns[:, 4:5], mhalf[:, 0:1], ALU.pow, eng="pool")

        def vg_norm(bg):
            P.stt(th, bank[bg], lns[:, 2:3], sgu[:, 0, :], ALU.subtract, ALU.mult)
            P.stt(vnb, th, lns[:, 5:6], sgu[:, 1, :], ALU.mult, ALU.add)

        def vg_B():
            bm = nb()
            for gg in range(4):
                P.mm(bank[bm][:, gg * 128:(gg + 1) * 128], wsp[:, gg, :], vnb[:, gg * 128:(gg + 1) * 128])
            for gg in range(4):
                P.stt(u_sb[:, gg * 128:(gg + 1) * 128], bank[bm][:, gg * 128:(gg + 1) * 128], bsp[:, gg:gg + 1],
                      u_sb[:, gg * 128:(gg + 1) * 128], ALU.add, ALU.mult)

        zsp = wspf.re("p a d -> p (a d)")

        def zs_imm(bs):
            P.actf(th, bank[bs], AF.Tanh, scale=0.5)
            P.stt(zsp, th, 1.0, bank[bs], ALU.add, ALU.mult)

        def zs_mul():
            P.tt(s_bf, zsp, u_sb, ALU.mult)

        def zs_B(tt):
            def f():
                b = nb()
                for j in range(4):
                    P.tr(bankb[b][:, j * 128:(j + 1) * 128], s_bf[:, j * 128:(j + 1) * 128], identb)
                P.copy(catT[:, 4:8, tt * 128:(tt + 1) * 128], bankb[b][:, 0:512].re("p (h n) -> p h n", h=4), eng="act")
            return f

        pending_inject = []

        def inject():
            if pending_inject:
                pending_inject.pop(0)()

        B_load(0)
        for ci, (g, c) in enumerate(chunks):
                npieces = NK[g] // PIECE
                xsb = xs[ci % 2]
                hT = hTs[ci % 2]
                if ci == 0:
                    rms_part1(xsb, ropes[ci % 2], ABs[ci % 2], g, 0)
                    rms_part2(hT, g)
                for tt in range(4):
                    bg = proj(hT, tt, Wqr, 1536)

                    def vg_step(bg=bg):
                        vg_imm(bg)
                        later(1, lambda: vg_norm(bg))
                        later(4, vg_B)
                    step(vg_step)
                    inject()
                    bq = proj(hT, tt, Wqr, 0)

                    def q_step(bq=bq, tt=tt):
                        ctx = rope_front(bq, ABs[ci % 2][:, tt], t1)
                        later(1, lambda: rope_back(ctx))
                        qb = q_B(tt)
                        later(3, lambda: qb(ctx["out"]))
                    step(q_step)
                    inject()
                    bz = proj(hT, tt, Wqr, 512)
                    step(lambda bz=bz, tt=tt: za_imm(tt, bz))
                    inject()
                    bu = proj(hT, tt, Wqr, 1024)
                    step(lambda bu=bu: P.actf(u_sb, bank[bu], AF.Identity, scale=0.5))
                    inject()
                    bs = proj(hT, tt, Wqr, 2048)

                    def zs_step(bs=bs, tt=tt):
                        zs_imm(bs)
                        later(1, zs_mul)
                        later(3, zs_B(tt))
                    step(zs_step)
                    inject()
                flush()
                while pending_inject:
                    inject()
                if ci + 1 < len(chunks):
                    B_load(ci + 1)

                accA, accB, den, misc = bank[4], bank[5], bank[6], bank[7]
                pieces = [(h, pc) for h in range(4) for pc in range(npieces)]
                slot_of = {}

                def load_piece(j):
                    h, pc = pieces[j]
                    s = pcnt[0] % NSLOT
                    pcnt[0] += 1
                    slot_of[j] = s
                    P.dma(ringK[s], DR(kt_scr[g][h, :, pc * PIECE:(pc + 1) * PIECE], f"kts{g}_{pc}r"), partial=False)
                    P.dma(ringV[s], DR(v_scr[g][h, :, pc * TPP:(pc + 1) * TPP, :], f"vs{g}_{pc}r"), partial=False)

                iters = [(j, t) for j in range(len(pieces)) for t in range(TPP)]
                PF = 2
                for j in range(min(PF, len(pieces))):
                    load_piece(j)

                def qk(i):
                    j, t = iters[i]
                    h = pieces[j][0]
                    s = slot_of[j]
                    S = Sps[i % 2]
                    P.mm(S[:, 0:512], ringK[s][0:64, t * 128:(t + 1) * 128], QT[0:64, h, :], tp=(0, 0))
                    P.mm(S[:, 512:1024], ringK[s][64:128, t * 128:(t + 1) * 128], QT[64:128, h, :], tp=(64, 0))

                qk(0)
                if len(iters) > 1:
                    qk(1)
                for i, (j, t) in enumerate(iters):
                    h, pc = pieces[j]
                    if t == 0 and j + PF < len(pieces):
                        load_piece(j + PF)
                    s = slot_of[j]
                    pt = PT[i % 4]
                    P.actf(pt, Sps[i % 2], AF.Exp, bias=negB[:, 0:1], scale=1.0)
                    if i + 2 < len(iters):
                        qk(i + 2)
                    first = (pc == 0 and t == 0)
                    last = (pc == npieces - 1 and t == TPP - 1)
                    P.mm(accA, ringV[s][:, t, :], pt[:, 0:512], start=first, stop=last)
                    P.mm(accB, ringV[s][:, t, :], pt[:, 512:1024], start=first, stop=last)
                    if t % 2 == 1:
                        pp_ = PT[(i - 1) % 4]
                        dfirst = (pc == 0 and t == 1)
                        P.mm(den[0:32, :], ones_bf, pp_[:, 0:512], start=dfirst, stop=last, tp=(0, 0))
                        P.mm(den[32:64, :], ones_bf, pp_[:, 512:1024], start=dfirst, stop=last, tp=(0, 32))
                        P.mm(den[64:96, :], ones_bf, pt[:, 0:512], start=dfirst, stop=last, tp=(0, 64))
                        P.mm(den[96:128, :], ones_bf, pt[:, 512:1024], start=dfirst, stop=last, tp=(0, 96))
                    if h == 3 and pc == 0 and t == 0 and ci + 1 < len(chunks):
                        rms_part1a(xs[(ci + 1) % 2], ropes[(ci + 1) % 2], ABs[(ci + 1) % 2], chunks[ci + 1][0], 0)
                    if h == 3 and pc == 0 and t == 4 and ci + 1 < len(chunks):
                        for tt_ in range(4):
                            P.ts(xn[:, tt_, :], xs[(ci + 1) % 2][:, tt_, :], rs4[:, tt_:tt_ + 1], None, op0=ALU.mult)
                    if last:
                        P.copy(eA, accA); P.copy(eB, accB, eng="act"); P.copy(eD, den)
                        for qb in range(4):
                            P.tr(misc[:, qb * 128:(qb + 1) * 128], eD[:, qb * 128:(qb + 1) * 128], ident)
                        for qb in range(4):
                            P.tr(accA[:, qb * 128:(qb + 1) * 128], eA[:, qb * 128:(qb + 1) * 128], ident)
                        for qb in range(4):
                            P.tr(accB[:, qb * 128:(qb + 1) * 128], eB[:, qb * 128:(qb + 1) * 128], ident)
                        dT2 = misc.re("p (q h c) -> p q c h", q=4, h=2)
                        P.red(rl[:, :, 0], dT2[:, :, 0, :], op=ALU.add)
                        P.red(rl[:, :, 1], dT2[:, :, 32, :], op=ALU.add)
                        P.recip(rl, rl)
                        P.ts(rl[:, :, 1:2], rl[:, :, 1:2], neglam[:, 0:1], None, op0=ALU.mult)
                        e3 = eA.re("p (q v) -> p q v", q=4); f3 = eB.re("p (q v) -> p q v", q=4)
                        P.tt(e3, accA.re("p (q v) -> p q v", q=4), V(rl.ap[:, :, 0:1].broadcast_to([128, 4, 128]), rl.res), ALU.mult)
                        P.tt(f3, accB.re("p (q v) -> p q v", q=4), V(rl.ap[:, :, 1:2].broadcast_to([128, 4, 128]), rl.res), ALU.mult)
                        P.tt(eA, eA, eB, ALU.add)
                        P.tt(eB, eA, eA, ALU.mult)
                        P.red(ss4, f3, op=ALU.add)
                        P.ts(ss4, ss4, float(128 * EPS), None, op0=ALU.add)
                        P.tt(rr4, ss4, mhalf[:, 0:4], ALU.pow, eng="pool")
                        P.tt(e3, e3, bc(rr4, [128, 4, 128], 2), ALU.mult)
                        P.tt(o_store[:, :, h * 128:(h + 1) * 128], e3, bc(subw, [128, 4, 128], 1), ALU.mult)
                        if h == 3 and ci + 1 < len(chunks):
                            sets["h"] = [0, 1, 2, 3]
                            rms_part2(hTs[(ci + 1) % 2], chunks[ci + 1][0])
                            sets["h"] = [6, 7]

                for tt in range(4):
                    kr = krs[tt % 2]
                    P.tt(kr, o_store[:, tt, :], za[:, tt, :], ALU.mult)
                    b = nb()
                    for j in range(4):
                        P.tr(bankb[b][:, j * 128:(j + 1) * 128], kr[:, j * 128:(j + 1) * 128], identb)
                    P.copy(catT[:, 0:4, tt * 128:(tt + 1) * 128], bankb[b][:, 0:512].re("p (h n) -> p h n", h=4), eng="act")
                inj = []
                for tt in range(4):
                    for hf in range(2):
                        def grp(tt=tt, hf=hf, g=g, xsb=xsb):
                            b = nb("p")
                            for k in range(8):
                                P.mm(bank[b], catT[:, k, tt * 128:(tt + 1) * 128], Wout[:, k, hf * 512:(hf + 1) * 512], start=(k == 0), stop=(k == 7))
                            tb = eA if hf == 0 else eB
                            P.tt(tb, bank[b], gate_bc[:, g, hf * 512:(hf + 1) * 512], ALU.mult)
                            P.tt(xsb[:, tt, hf * 512:(hf + 1) * 512], xsb[:, tt, hf * 512:(hf + 1) * 512], tb, ALU.add, eng="pool")
                        inj.append(grp)

                def ydma(g=g, c=c, xsb=xsb):
                    P.dma(DR(y_d[g][c * 512:(c + 1) * 512, :].rearrange("(t p) d -> p t d", p=128), "y"), xsb, eng="pool")
                inj.append(ydma)
                if ci + 1 < len(chunks):
                    pending_inject.extend(inj)
                else:
                    for f_ in inj:
                        f_()
        P.emit()
    return nc


def rope_tables(S):
    half = 32
    inv_freq = (1.0 / (np.float32(10000.0) ** (np.arange(half, dtype=np.float32) / np.float32(half)))).astype(np.float32)
    ang = (np.arange(S, dtype=np.float32)[:, None] * inv_freq[None, :]).astype(np.float32)
    ang = np.concatenate([ang, ang], axis=-1)
    cos = np.cos(ang.astype(np.float64)).astype(np.float32)
    sin = np.sin(ang.astype(np.float64)).astype(np.float32)
    sin[:, :half] *= -1.0
    return np.ascontiguousarray(np.stack([cos, sin], axis=1))


def prep_inputs(SP, SS, x_prompt, x_sample, c_prompt, c_sample, norm_w, w_ada, b_ada, w_in, w_out,
                q_norm_w, k_norm_w, lambda_qk, subln_w, sgu_norm_w, sgu_norm_b, w_spatial, b_spatial):
    f = lambda a: np.ascontiguousarray(np.asarray(a, dtype=np.float32))
    NQP, NQS = SP // 8, SS // 2
    rope = rope_tables(max(SP, SS))
    common = {
        "rope_k": rope,
        "w_ada": f(w_ada[0]), "b_adaT": f(np.asarray(b_ada[0]).reshape(24, 128).T),
        "norm_wT": f(np.asarray(norm_w[0]).reshape(8, 128).T),
        "w_in": f(w_in[0]), "w_out": f(w_out[0]),
        "qkw": f(np.concatenate([np.asarray(q_norm_w[0]), np.asarray(k_norm_w[0])])),
        "lamqk": f(np.asarray(lambda_qk[0]).reshape(-1)),
        "subw": f(subln_w[0]),
        "sgunw": f(np.concatenate([np.asarray(sgu_norm_w[0]), np.asarray(sgu_norm_b[0])])),
        "wspT": f(np.asarray(w_spatial[0]).transpose(2, 0, 1)),
        "bspT": f(np.asarray(b_spatial[0]).T),
        "ident": np.eye(128, dtype=np.float32),
        "xkv_p": f(x_prompt[0]),
    }
    maps = []
    for c in range(N_CORES):
        b, hf = c // 2, c % 2
        m = dict(common)
        m["xkv_s"] = f(x_sample[b])
        m["xq_p"] = f(x_prompt[0][c * NQP:(c + 1) * NQP])
        m["xq_s"] = f(x_sample[b][hf * NQS:(hf + 1) * NQS])
        m["rope_qp"] = np.ascontiguousarray(rope[c * NQP:(c + 1) * NQP])
        m["rope_qs"] = np.ascontiguousarray(rope[hf * NQS:(hf + 1) * NQS])
        cc = np.stack([np.asarray(c_prompt[0]), np.asarray(c_sample[b])], axis=-1)
        m["cT"] = f(cc.reshape(8, 128, 2).transpose(1, 0, 2))
        maps.append(m)
    return maps


_NC_CACHE = {}


def run(SP, SS, **inputs):
    key = (SP, SS)
    if key not in _NC_CACHE:
        _NC_CACHE[key] = build(SP, SS)
    nc = _NC_CACHE[key]
    maps = prep_inputs(SP, SS, **inputs)
    res = run_bass_kernel_spmd(nc, maps, core_ids=list(range(N_CORES)))
    NQP, NQS = SP // 8, SS // 2
    y_p = np.concatenate([res.results[c]["y_p"] for c in range(N_CORES)], axis=0)[None]
    y_s = np.stack([np.concatenate([res.results[2 * b]["y_s"], res.results[2 * b + 1]["y_s"]], axis=0) for b in range(4)], axis=0)
    return y_p.astype(np.float32), y_s.astype(np.float32)


def kernel(**inputs):
    inputs = {k: np.asarray(v) for k, v in inputs.items()}
    return run(16384, 8192, **inputs)
```
